# Optimizing a Trainium2 kernel written in Bass

```python
import jax, jax.numpy as jnp
from jax import lax
import numpy as np

D_MODEL = 1024
BATCH = 16
SEQ = 2048
DEPTH = 2

N_A_LAYERS = DEPTH // 2
N_B_LAYERS = DEPTH - N_A_LAYERS

A_HEADS = 16
A_HEAD_DIM = D_MODEL // A_HEADS
DILATED_PATTERNS = ((128, 1), (512, 4), (2048, 16))
BAND = max(w // d for w, d in DILATED_PATTERNS)

B_HEADS = 16
B_NOPE_DIM = 64
B_ROPE_DIM = 32
B_V_DIM = 64
Q_RANK = 384
KV_RANK = 256
ROPE_BASE = 10000.0
Q_BLOCK = 128

D_FF = 2816
CONV_WIDTH = 3

ALPHA = (2.0 * DEPTH) ** 0.25
BETA = (8.0 * DEPTH) ** -0.25
LN_EPS = 1e-5
RMS_EPS = 1e-6

kernel_name = 'yoco_dilated_mla_convffn_deepnorm'


def layer_norm(x, g, b):
    xf = x.astype(jnp.float32)
    mu = jnp.mean(xf, axis=-1, keepdims=True)
    var = jnp.mean(jnp.square(xf - mu), axis=-1, keepdims=True)
    return ((xf - mu) * lax.rsqrt(var + LN_EPS) * g.astype(jnp.float32) + b.astype(jnp.float32)).astype(x.dtype)


def rms_norm(x, g):
    xf = x.astype(jnp.float32)
    ms = jnp.mean(jnp.square(xf), axis=-1, keepdims=True)
    return (xf * lax.rsqrt(ms + RMS_EPS) * g.astype(jnp.float32)).astype(x.dtype)


def alibi_slopes(n_heads):
    return jnp.asarray([2.0 ** (-8.0 * (h + 1) / n_heads) for h in range(n_heads)], dtype=jnp.float32)


def rope_tables(seq):
    inv_freq = ROPE_BASE ** (-jnp.arange(0, B_ROPE_DIM, 2, dtype=jnp.float32) / B_ROPE_DIM)
    ang = jnp.arange(seq, dtype=jnp.float32)[:, None] * inv_freq[None, :]
    return jnp.cos(ang), jnp.sin(ang)


def apply_rope(t, cos, sin):
    tf = t.astype(jnp.float32)
    t1, t2 = jnp.split(tf, 2, axis=-1)
    return jnp.concatenate([t1 * cos - t2 * sin, t2 * cos + t1 * sin], axis=-1).astype(t.dtype)


def dilated_branch(q, k, v, window, dilation, slopes):
    bsz, seq, heads, dh = q.shape
    sub_len = seq // dilation
    n_blk = -(-sub_len // BAND)
    sub_pad = n_blk * BAND
    steps = window // dilation

    def to_sub(t):
        t = t.reshape(bsz, sub_len, dilation, heads, dh)
        return jnp.pad(t, ((0, 0), (0, sub_pad - sub_len), (0, 0), (0, 0), (0, 0)))

    def band(t):
        t = jnp.pad(t, ((0, 0), (BAND, 0), (0, 0), (0, 0), (0, 0)))
        t = t.reshape(bsz, n_blk + 1, BAND, dilation, heads, dh)
        return jnp.concatenate([t[:, :-1], t[:, 1:]], axis=2)

    qb = to_sub(q).reshape(bsz, n_blk, BAND, dilation, heads, dh)
    kb = band(to_sub(k))
    vb = band(to_sub(v))

    s = jnp.einsum('bnqrhd,bnkrhd->bnrhqk', qb, kb).astype(jnp.float32) * (dh ** -0.5)
    qi = jnp.arange(BAND)[:, None]
    ki = jnp.arange(2 * BAND)[None, :]
    back = BAND + qi - ki
    kpos = jnp.arange(n_blk)[:, None, None] * BAND - BAND + ki[None]
    valid = (back >= 0) & (back <= steps) & (kpos >= 0)
    bias = -slopes[:, None, None] * (dilation * back).astype(jnp.float32)[None]
    s = jnp.where(valid[:, None, None], s + bias, -jnp.inf)
    m = jnp.max(s, axis=-1, keepdims=True)
    p = jnp.exp(s - m)
    den = jnp.sum(p, axis=-1)
    lse = m[..., 0] + jnp.log(den)
    o = jnp.einsum('bnrhqk,bnkrhd->bnqrhd', p.astype(vb.dtype), vb).astype(jnp.float32)
    den_t = jnp.transpose(den, (0, 1, 4, 2, 3))
    o = o / den_t[..., None]
    o = o.reshape(bsz, sub_pad, dilation, heads, dh)[:, :sub_len].reshape(bsz, seq, heads, dh)
    lse = jnp.transpose(lse, (0, 1, 4, 2, 3)).reshape(bsz, sub_pad, dilation, heads)[:, :sub_len]
    return o, lse.reshape(bsz, seq, heads)


def dilated_mixer(x, w_qkv, w_o):
    bsz, seq, _ = x.shape
    qkv = (x @ w_qkv).reshape(bsz, seq, 3, A_HEADS, A_HEAD_DIM)
    q, k, v = qkv[:, :, 0], qkv[:, :, 1], qkv[:, :, 2]
    slopes = alibi_slopes(A_HEADS)
    outs, lses = [], []
    for window, dilation in DILATED_PATTERNS:
        o, l = dilated_branch(q, k, v, window, dilation, slopes)
        outs.append(o)
        lses.append(l)
    wts = jax.nn.softmax(jnp.stack(lses, axis=0), axis=0)
    o = jnp.einsum('pbsh,pbshd->bshd', wts, jnp.stack(outs, axis=0))
    return o.astype(x.dtype).reshape(bsz, seq, A_HEADS * A_HEAD_DIM) @ w_o


def shared_latent_kv(x, w_dkv, kv_norm_g, w_kr, w_uk, w_uv, cos, sin):
    bsz, seq, _ = x.shape
    c_kv = rms_norm(x @ w_dkv, kv_norm_g)
    k_nope = (c_kv @ w_uk).reshape(bsz, seq, B_HEADS, B_NOPE_DIM)
    v = (c_kv @ w_uv).reshape(bsz, seq, B_HEADS, B_V_DIM)
    k_rope = apply_rope(x @ w_kr, cos[None], sin[None])
    return k_nope, k_rope, v


def mla_mixer(x, w_dq, q_norm_g, w_uq, w_o, k_nope, k_rope, v, cos, sin):
    bsz, seq, _ = x.shape
    q = (rms_norm(x @ w_dq, q_norm_g) @ w_uq).reshape(bsz, seq, B_HEADS, B_NOPE_DIM + B_ROPE_DIM)
    q_nope = q[..., :B_NOPE_DIM]
    q_rope = apply_rope(q[..., B_NOPE_DIM:], cos[None, :, None], sin[None, :, None])
    n_q = seq // Q_BLOCK
    qn_blocks = q_nope.reshape(bsz, n_q, Q_BLOCK, B_HEADS, B_NOPE_DIM).transpose(1, 0, 2, 3, 4)
    qr_blocks = q_rope.reshape(bsz, n_q, Q_BLOCK, B_HEADS, B_ROPE_DIM).transpose(1, 0, 2, 3, 4)
    kpos = jnp.arange(seq)
    scale = (B_NOPE_DIM + B_ROPE_DIM) ** -0.5

    def attend(args):
        qn, qr, blk = args
        s = (jnp.einsum('bqhd,bkhd->bhqk', qn, k_nope)
             + jnp.einsum('bqhr,bkr->bhqk', qr, k_rope)).astype(jnp.float32) * scale
        qpos = blk * Q_BLOCK + jnp.arange(Q_BLOCK)
        s = jnp.where(kpos[None, :] <= qpos[:, None], s, -jnp.inf)
        p = jax.nn.softmax(s, axis=-1)
        return jnp.einsum('bhqk,bkhd->bqhd', p.astype(v.dtype), v)

    o = lax.map(attend, (qn_blocks, qr_blocks, jnp.arange(n_q)))
    o = o.transpose(1, 0, 2, 3, 4).reshape(bsz, seq, B_HEADS * B_V_DIM)
    return o @ w_o


def conv_ffn(x, w_in, conv_w, conv_b, w_out):
    seq = x.shape[1]
    u = x @ w_in
    up = jnp.pad(u, ((0, 0), (CONV_WIDTH - 1, 0), (0, 0)))
    c = conv_b
    for j in range(CONV_WIDTH):
        c = c + conv_w[j] * up[:, j:j + seq]
    gate, val = jnp.split(c, 2, axis=-1)
    return (jax.nn.silu(gate) * val) @ w_out


def setup_inputs(seed: int = 0) -> dict:
    key = jax.random.key(seed)
    ks = iter(jax.random.split(key, 32))

    def dense(shape, fan_in, scale=1.0):
        return jax.random.normal(next(ks), shape, jnp.float32) * (scale * fan_in ** -0.5)

    def gain(shape):
        return 1.0 + 0.02 * jax.random.normal(next(ks), shape, jnp.float32)

    def bias(shape):
        return 0.02 * jax.random.normal(next(ks), shape, jnp.float32)

    hd_a = A_HEADS * A_HEAD_DIM
    v_scale = jnp.concatenate([jnp.ones((2 * hd_a,), jnp.float32), jnp.full((hd_a,), BETA, jnp.float32)])
    return {
        'x': jax.random.normal(next(ks), (BATCH, SEQ, D_MODEL), jnp.float32),
        'a_w_qkv': dense((N_A_LAYERS, D_MODEL, 3 * hd_a), D_MODEL) * v_scale,
        'a_w_o': dense((N_A_LAYERS, hd_a, D_MODEL), hd_a, BETA),
        'kv_w_dkv': dense((D_MODEL, KV_RANK), D_MODEL),
        'kv_norm_g': gain((KV_RANK,)),
        'kv_w_kr': dense((D_MODEL, B_ROPE_DIM), D_MODEL),
        'kv_w_uk': dense((KV_RANK, B_HEADS * B_NOPE_DIM), KV_RANK),
        'kv_w_uv': dense((KV_RANK, B_HEADS * B_V_DIM), KV_RANK, BETA),
        'b_w_dq': dense((N_B_LAYERS, D_MODEL, Q_RANK), D_MODEL),
        'b_q_norm_g': gain((N_B_LAYERS, Q_RANK)),
        'b_w_uq': dense((N_B_LAYERS, Q_RANK, B_HEADS * (B_NOPE_DIM + B_ROPE_DIM)), Q_RANK),
        'b_w_o': dense((N_B_LAYERS, B_HEADS * B_V_DIM, D_MODEL), B_HEADS * B_V_DIM, BETA),
        'ffn_w_in': dense((DEPTH, D_MODEL, 2 * D_FF), D_MODEL),
        'ffn_conv_w': dense((DEPTH, CONV_WIDTH, 2 * D_FF), CONV_WIDTH),
        'ffn_conv_b': bias((DEPTH, 2 * D_FF)),
        'ffn_w_out': dense((DEPTH, D_FF, D_MODEL), D_FF, BETA),
        'ln_mix_g': gain((DEPTH, D_MODEL)),
        'ln_mix_b': bias((DEPTH, D_MODEL)),
        'ln_ffn_g': gain((DEPTH, D_MODEL)),
        'ln_ffn_b': bias((DEPTH, D_MODEL)),
    }


def reference(x, a_w_qkv, a_w_o, kv_w_dkv, kv_norm_g, kv_w_kr, kv_w_uk, kv_w_uv,
              b_w_dq, b_q_norm_g, b_w_uq, b_w_o, ffn_w_in, ffn_conv_w, ffn_conv_b, ffn_w_out,
              ln_mix_g, ln_mix_b, ln_ffn_g, ln_ffn_b):
    cos, sin = rope_tables(x.shape[1])
    k_nope = k_rope = v_shared = None
    for layer in range(DEPTH):
        if layer == N_A_LAYERS:
            k_nope, k_rope, v_shared = shared_latent_kv(x, kv_w_dkv, kv_norm_g, kv_w_kr,
                                                        kv_w_uk, kv_w_uv, cos, sin)
        if layer < N_A_LAYERS:
            mix = dilated_mixer(x, a_w_qkv[layer], a_w_o[layer])
        else:
            j = layer - N_A_LAYERS
            mix = mla_mixer(x, b_w_dq[j], b_q_norm_g[j], b_w_uq[j], b_w_o[j],
                            k_nope, k_rope, v_shared, cos, sin)
        x = layer_norm(ALPHA * x + mix, ln_mix_g[layer], ln_mix_b[layer])
        f = conv_ffn(x, ffn_w_in[layer], ffn_conv_w[layer], ffn_conv_b[layer], ffn_w_out[layer])
        x = layer_norm(ALPHA * x + f, ln_ffn_g[layer], ln_ffn_b[layer])
    return x
```

```python
import math
import numpy as np
import ml_dtypes
import concourse.bass as bass
import concourse.mybir as mybir
from concourse.bass_utils import run_bass_kernel_spmd

F32 = mybir.dt.float32
BF16 = mybir.dt.bfloat16
AF = mybir.ActivationFunctionType
ALU = mybir.AluOpType

NCORES = 8
SEQ = 2048
DM = 1024
NSEQ = 2
NT = SEQ // 128
H = 16
DFF = 2816
NG = DFF // 128
ALPHA = 4.0 ** 0.25
LN_EPS = 1e-5
RMS_EPS = 1e-6
NEG = -1.0e9


class Prog:
    ENG = ("pe", "act", "dve", "pool", "sp")

    def __init__(self, nc):
        self.nc = nc
        self.q = {e: [] for e in self.ENG}
        self.esem = {}
        self.ecnt = {}
        self.waited = {e: {} for e in self.ENG}
        self.res = {}
        self.dsem = {}
        self.pending = {e: [] for e in self.ENG}
        self.last = {e: None for e in self.ENG}
        self.gen = 0
        self.new_engine_sems()

    def new_engine_sems(self):
        for e in self.ENG:
            if e == "sp":
                continue
            self.esem[e] = self.nc.alloc_semaphore(name=f"es_{e}_{self.gen}")
            self.ecnt[e] = 0
        self.gen += 1

    def _waits(self, eng, deps):
        waits = {}
        for tok in deps:
            sem, val, teng = tok
            if eng == "pe" and teng == "pe":
                continue
            assert val is not None, "dependency on unresolved (non-signalled) op"
            k = id(sem)
            if self.waited[eng].get(k, 0) >= val:
                continue
            if k not in waits or waits[k][1] < val:
                waits[k] = (sem, val)
        for k, (s, v) in waits.items():
            self.waited[eng][k] = v
        return list(waits.values())

    def op(self, eng, fn, reads=(), writes=(), signal=True, dma=None):
        deps = []
        for r in reads:
            st = self.res.get(r)
            if st is not None and st[0] is not None:
                deps.append(st[0])
        for w in writes:
            st = self.res.get(w)
            if st is not None:
                if st[0] is not None:
                    deps.append(st[0])
                deps.extend(st[1])
        waits = self._waits(eng, deps)
        if dma is not None:
            key, n = dma
            ent = self.dsem.get(key)
            if ent is None:
                ent = [self.nc.alloc_semaphore(name=f"ds_{len(self.dsem)}"), 0]
                self.dsem[key] = ent
            ent[1] += 16 * n
            tok = [ent[0], ent[1], "dma"]
            inc = (ent[0], 16)
        elif signal:
            self.ecnt[eng] += 1
            tok = [self.esem[eng], self.ecnt[eng], eng]
            inc = (self.esem[eng], 1)
            for p in self.pending[eng]:
                p[0], p[1], p[2] = tok[0], tok[1], tok[2]
            self.pending[eng] = []
            self.last[eng] = tok
        else:
            tok = [None, None, eng]
            inc = None
            self.pending[eng].append(tok)
        self.q[eng].append((waits, fn, inc))
        for r in reads:
            self.res.setdefault(r, [None, []])[1].append(tok)
        for w in writes:
            self.res[w] = [tok, []]
        return tok

    def barrier(self):
        for e in self.ENG:
            assert not self.pending[e], f"pending unsignalled ops on {e} at barrier"
        toks = [t for t in self.last.values() if t is not None]
        toks += [[ent[0], ent[1], "dma"] for ent in self.dsem.values() if ent[1] > 0]
        for e in self.ENG:
            w = self._waits(e, toks)
            if w:
                self.q[e].append((w, None, None))
        self.res = {}

    def emit(self, block):
        m = {"pe": block.tensor, "act": block.scalar, "dve": block.vector,
             "pool": block.gpsimd, "sp": block.sync}
        for e in self.ENG:
            def body(eng, e=e):
                for waits, fn, inc in self.q[e]:
                    for (s, v) in waits:
                        eng.wait_ge(s, v)
                    if fn is None:
                        continue
                    ins = fn(eng)
                    if inc is not None:
                        if isinstance(ins, (list, tuple)):
                            for i in ins:
                                i.then_inc(inc[0], inc[1])
                        else:
                            ins.then_inc(inc[0], inc[1])
            m[e](body)


class Arena:
    def __init__(self, base, words):
        self.base = base
        self.words = words
        self.top = 0

    def mark(self):
        return self.top

    def release(self, m):
        self.top = m

    def alloc(self, shape, dtype):
        n = 1
        for s in shape:
            n *= s
        nbytes = n * (2 if dtype == BF16 else 4)
        w = (nbytes + 3) // 4
        w = (w + 7) // 8 * 8
        assert self.top + w <= self.words, f"SBUF arena overflow: need {self.top + w} have {self.words}"
        ap = self.base[:, self.top:self.top + w]
        self.top += w
        if dtype == BF16:
            ap = ap.bitcast(BF16)
        ap = ap[:, 0:n]
        if len(shape) == 1:
            return ap
        names = " ".join(f"d{i}" for i in range(len(shape)))
        kw = {f"d{i}": shape[i] for i in range(1, len(shape))}
        return ap.rearrange(f"p ({names}) -> p {names}", **kw)


def alibi_slope(h):
    return 2.0 ** (-8.0 * (h + 1) / H)


def build_program(stop_after=None, debug_out=False, skip_ab=False, c1_parts=5):
    nc = bass.Bass("TRN2", target_bir_lowering=False)
    dt = lambda name, shape, dtype=F32, kind="ExternalInput": nc.dram_tensor(name, shape, dtype, kind=kind)
    x_d = dt("x", [NSEQ * SEQ, DM]).ap()
    wqkv_d = dt("a_w_qkv", [DM, 3 * DM]).ap()
    awo_d = dt("a_w_o", [DM, DM]).ap()
    wdkv_d = dt("kv_w_dkv", [DM, 256]).ap()
    gkv_t = dt("kv_norm_g", [1, 256])
    wkr_d = dt("kv_w_kr", [DM, 32]).ap()
    wuk_d = dt("kv_w_uk", [256, DM]).ap()
    wuv_d = dt("kv_w_uv", [256, DM]).ap()
    wdq_d = dt("b_w_dq", [DM, 384]).ap()
    gq_t = dt("b_q_norm_g", [1, 384])
    wuq_d = dt("b_w_uq", [384, 1536]).ap()
    bwo_d = dt("b_w_o", [DM, DM]).ap()
    win_d = dt("ffn_w_in", [2, DM, 2 * DFF]).ap()
    cw_d = dt("ffn_conv_w", [128, 2 * 2 * NG * 3]).ap()
    cb_d = dt("ffn_conv_b", [128, 2 * 2 * NG]).ap()
    wout_d = dt("ffn_w_out", [2, DFF, DM]).ap()
    lng_t = {}
    for nm in ("ln_mix_g", "ln_mix_b", "ln_ffn_g", "ln_ffn_b"):
        lng_t[nm] = dt(nm, [2, DM])
    ident_d = dt("c_ident", [128, 128], BF16).ap()
    dtab_d = dt("c_dtab", [128, SEQ]).ap()
    ltab_d = dt("c_ltab", [128, SEQ]).ap()
    cmaskb_d = dt("c_cmaskb", [128, 128], BF16).ap()
    ropec_d = dt("c_ropec", [32, SEQ]).ap()
    ropes_d = dt("c_ropes", [32, SEQ]).ap()
    y_d = dt("y", [NSEQ * SEQ, DM], F32, "ExternalOutput").ap()
    xr = [dt(f"xr{i}", [NSEQ * SEQ, DM], F32, "Internal").ap() for i in range(3)]
    dbg_d = dt("dbg", [SEQ, DM], BF16, "ExternalOutput").ap() if debug_out else None

    def bcast_rows(t, row, n):
        return bass.AP(t, row * n, [(0, 128), (1, n)])

    WORDS = 53200
    sb = nc.sbuf_tensor("arena", [128, WORDS], F32)
    sbuf = sb.__enter__()
    psc = nc.psum_tensor("psum", [128, 4096], F32)
    ps = psc.__enter__()
    blockc = nc.Block()
    block = blockc.__enter__()

    P = Prog(nc)
    A = Arena(sbuf, WORDS)

    def bank(b, n=512):
        return ps[:, b * 512:b * 512 + n]

    def bank_bf(b):
        return ps[:, b * 512:(b + 1) * 512].bitcast(BF16)

    ident = A.alloc([128], BF16)
    cw = A.alloc([2, 2 * NG, 3], F32)
    cb = A.alloc([2, 2 * NG], F32)
    gkv_bc = A.alloc([256], F32)
    gq_bc = A.alloc([384], F32)
    cmaskb = A.alloc([128], BF16)
    halo = A.alloc([2 * NG, 2], F32)
    xTb = A.alloc([8, SEQ], BF16)
    wdkv = A.alloc([8, 256], BF16)
    wdq = A.alloc([8, 384], BF16)
    wkr = A.alloc([8, 32], BF16)
    wkr_rot = A.alloc([8, 32], BF16)

    def ld(dst, src, key, eng="sp", n=1):
        P.op(eng, lambda e: e.dma_start(out=dst, in_=src), writes=[key], dma=(key, 1))

    ld(ident, ident_d, "ident")
    ld(cw.rearrange("p l c j -> p (l c j)"), cw_d, "cw")
    ld(cb.rearrange("p l c -> p (l c)"), cb_d, "cb")
    ld(gkv_bc, bcast_rows(gkv_t, 0, 256), "gkv")
    ld(gq_bc, bcast_rows(gq_t, 0, 384), "gq")
    ld(cmaskb, cmaskb_d, "cmaskb")

    def xkeys(t0, t1):
        return [("xTb", t) for t in range(t0 // 128, (t1 + 127) // 128)]

    def mm(out, lhsT, rhs, start, stop, reads, writes, signal=True, **kw):
        P.op("pe", lambda e: e.matmul(out, lhsT=lhsT, rhs=rhs, start=start, stop=stop, **kw),
             reads=reads, writes=writes, signal=signal)

    def tr(out, in_, reads, writes, signal=True):
        P.op("pe", lambda e: e.transpose(out=out, in_=in_, identity=ident), reads=list(reads) + ["ident"],
             writes=writes, signal=signal)

    def act(out, in_, func, reads, writes, **kw):
        P.op("act", lambda e: e.activation(out=out, in_=in_, func=func, **kw), reads=reads, writes=writes)

    def stt(out, in0, scalar, in1, op0, op1, reads, writes):
        P.op("dve", lambda e: e.scalar_tensor_tensor(out=out, in0=in0, scalar=scalar, in1=in1, op0=op0, op1=op1),
             reads=reads, writes=writes)

    def tt_(eng, out, in0, in1, op, reads, writes):
        P.op(eng, lambda e: e.tensor_tensor(out=out, in0=in0, in1=in1, op=op), reads=reads, writes=writes)

    def ts_(eng, out, in0, s1, s2, op0, op1, reads, writes):
        if op1 is None:
            P.op(eng, lambda e: e.tensor_scalar(out=out, in0=in0, scalar1=s1, scalar2=None, op0=op0),
                 reads=reads, writes=writes)
        else:
            P.op(eng, lambda e: e.tensor_scalar(out=out, in0=in0, scalar1=s1, scalar2=s2, op0=op0, op1=op1),
                 reads=reads, writes=writes)

    def cp(eng, out, in_, reads, writes):
        if eng == "act":
            act(out, in_, AF.Copy, reads, writes)
        else:
            P.op(eng, lambda e: e.tensor_copy(out=out, in_=in_), reads=reads, writes=writes)

    def recip(out, in_, reads, writes):
        P.op("dve", lambda e: e.reciprocal(out=out, in_=in_), reads=reads, writes=writes)

    def memset(eng, ap, val, reads, writes):
        P.op(eng, lambda e: e.memset(ap, val), reads=reads, writes=writes)

    def dma(eng, pairs, reads, writes, key):
        P.op(eng, lambda e: [e.dma_start(out=o, in_=i) for (o, i) in pairs], reads=reads, writes=writes,
             dma=(key, len(pairs)))

    def load_w(dst, src, key, parts=1):
        k = dst.shape[1]
        step = (k + parts - 1) // parts
        pairs = [(dst[:, a:min(k, a + step)], src[:, a:min(k, a + step)]) for a in range(0, k, step)]
        dma("pool", pairs, [], [key], key)

    def transpose_tile(src, nch, src_key, dsts, bank_id):
        pb = bank_bf(bank_id).rearrange("p (c n) -> p c n", n=128)
        for c in range(nch):
            tr(pb[:, c, :], src[:, c * 128:(c + 1) * 128], [src_key], [("ps", bank_id)], signal=(c == nch - 1))
        for (eng, dst, c0, n, wkeys) in dsts:
            cp(eng, dst, pb[:, c0:c0 + n, :], [("ps", bank_id)], wkeys)

    epsc = A.alloc([4], F32)
    epsln = epsc[:, 0:1]
    epsrms = epsc[:, 1:2]
    memset("pool", epsc[:, 0:1], LN_EPS, [], ["epsc0"])
    memset("pool", epsc[:, 1:2], RMS_EPS, ["epsc0"], ["epsc"])
    load_w(wdkv, wdkv_d.rearrange("(k p) c -> p k c", p=128), "wdkv")
    load_w(wdq, wdq_d.rearrange("(k p) c -> p k c", p=128), "wdq")
    load_w(wkr, wkr_d.rearrange("(k p) c -> p k c", p=128), "wkr")
    act(wkr_rot[:, :, 0:16], wkr[:, :, 16:32], AF.Identity, ["wkr"], ["wkr_rot_a"], scale=-1.0)
    act(wkr_rot[:, :, 16:32], wkr[:, :, 0:16], AF.Identity, ["wkr", "wkr_rot_a"], ["wkr_rot"])

    def ln_tiles_alloc():
        d = {}
        d["xres"] = [A.alloc([DM], F32) for _ in range(2)]
        d["z"] = [A.alloc([DM], F32) for _ in range(2)]
        d["xo"] = [A.alloc([DM], F32) for _ in range(2)]
        d["xb"] = [A.alloc([DM], BF16) for _ in range(2)]
        d["st"] = [A.alloc([2, 6], F32) for _ in range(2)]
        d["mv"] = [A.alloc([2], F32) for _ in range(2)]
        d["sd"] = [A.alloc([1], F32) for _ in range(2)]
        d["rstd"] = [A.alloc([1], F32) for _ in range(2)]
        d["g"] = A.alloc([DM], F32)
        d["b"] = A.alloc([DM], F32)
        return d

    def ln_load_params(L, gname, bname, layer):
        ld(L["g"], bcast_rows(lng_t[gname], layer, DM), "ln_g")
        ld(L["b"], bcast_rows(lng_t[bname], layer, DM), "ln_b")

    def load_xres(L, src_d, row0, slot):
        dma("sp", [(L["xres"][slot], src_d[row0:row0 + 128, :])], [], [("xres", slot)], ("xres", slot))

    def ln_stage1(L, slot, ybank):
        yps = ps[:, ybank * 512:ybank * 512 + 1024]
        xres, z = L["xres"][slot], L["z"][slot]
        st, mv, sd = L["st"][slot], L["mv"][slot], L["sd"][slot]
        stt(z, xres, ALPHA, yps, ALU.mult, ALU.add,
            [("xres", slot), ("ps", ybank), ("ps", ybank + 1)], [("z", slot)])
        P.op("dve", lambda e: e.bn_stats(out=st[:, 0, :], in_=z[:, 0:512]), reads=[("z", slot)],
             writes=[("st0", slot)])
        P.op("dve", lambda e: e.bn_stats(out=st[:, 1, :], in_=z[:, 512:1024]), reads=[("z", slot)],
             writes=[("st1", slot)])
        P.op("dve", lambda e: e.bn_aggr(out=mv, in_=st.rearrange("p a b -> p (a b)")),
             reads=[("st0", slot), ("st1", slot)], writes=[("mv", slot)])
        act(sd, mv[:, 1:2], AF.Sqrt, [("mv", slot), "epsc"], [("sd", slot)], bias=epsln, scale=1.0)

    def ln_stage2(L, slot, dst_d, row0, tt, do_transpose, tbank):
        z, xo, xb = L["z"][slot], L["xo"][slot], L["xb"][slot]
        mv, sd, rstd = L["mv"][slot], L["sd"][slot], L["rstd"][slot]
        recip(rstd, sd, [("sd", slot)], [("rstd", slot)])
        stt(z, z, mv[:, 0:1], L["g"], ALU.subtract, ALU.mult,
            [("z", slot), ("mv", slot), "ln_g"], [("z", slot)])
        stt(xo, z, rstd, L["b"], ALU.mult, ALU.add,
            [("z", slot), ("rstd", slot), "ln_b"], [("xo", slot)])
        dma("pool", [(dst_d[row0:row0 + 128, :], xo)], [("xo", slot)], [], ("xo_st", slot))
        if do_transpose:
            cp("act", xb, xo, [("xo", slot)], [("xb", slot)])
            dst = xTb[:, :, tt * 128:(tt + 1) * 128]
            transpose_tile(xb, 8, ("xb", slot), [("act", dst, 0, 8, [("xTb", tt)])], tbank)

    def oacc(qt):
        b = 2 + qt // 7
        c = (qt % 7) * 65
        return ps[:, b * 512 + c: b * 512 + c + 65]

    SBANKS0 = (0, 1, 5, 6)
    SBANKS1 = (0, 1, 5, 6)
    SKIP_T = 60.0

    def attention(h, QT, KT, qkey, kkey, scale, layer0, Th, thkey, V_aug, O_tm, Sb, PT, rden, hooks=()):
        chunks = []
        slope = alibi_slope(h)
        for kb in range(NT):
            for cc in range(kb // 4, 4):
                q0 = max(128 * kb, 512 * cc)
                N = 512 * (cc + 1) - q0
                if layer0:
                    while N > 0 and slope * (q0 + N - 128 - 128 * kb - 127) > SKIP_T:
                        N -= 128
                    if N <= 0:
                        continue
                chunks.append((kb, q0, N))
        n = len(chunks)
        started = set()
        SBANKS = SBANKS0 if layer0 else SBANKS1
        LA = len(SBANKS) - 1
        NS = len(SBANKS)

        def qk(i):
            kb, q0, N = chunks[i]
            sbk = SBANKS[i % NS]
            if (not layer0) and q0 == 128 * kb:
                mm(bank(sbk, N), KT[:, kb * 128:(kb + 1) * 128], QT[:, q0:q0 + N], True, False,
                   [qkey, kkey], [("ps", sbk)], signal=False)
                mm(bank(sbk, 128), ident, cmaskb, False, True, ["ident", "cmaskb"], [("ps", sbk)])
            elif layer0:
                c0 = q0 - 128 * kb
                mm(bank(sbk, N), KT[:, kb * 128:(kb + 1) * 128], QT[:, q0:q0 + N], True, False,
                   [qkey, kkey], [("ps", sbk)], signal=False)
                mm(bank(sbk, N), ident, Th[0][:, c0:c0 + N], False, False, ["ident", thkey], [("ps", sbk)],
                   signal=False)
                mm(bank(sbk, N), ident, Th[1][:, c0:c0 + N], False, True, ["ident", thkey], [("ps", sbk)])
            else:
                mm(bank(sbk, N), KT[:, kb * 128:(kb + 1) * 128], QT[:, q0:q0 + N], True, True,
                   [qkey, kkey], [("ps", sbk)])

        def mid(i):
            kb, q0, N = chunks[i]
            sbk = SBANKS[i % NS]
            s3 = i % NS
            S = bank(sbk, N)
            if layer0:
                act(PT[s3][:, 0:N], S, AF.Exp, [("ps", sbk)], [("PT", s3)], scale=scale)
            else:
                act(PT[s3][:, 0:N], S, AF.Exp, [("ps", sbk)], [("PT", s3)], scale=scale)

        def pv(i):
            kb, q0, N = chunks[i]
            s3 = i % NS
            nq = N // 128
            for j in range(nq):
                qt = q0 // 128 + j
                ob = 2 + qt // 7
                first = ob not in started
                started.add(ob)
                mm(oacc(qt), PT[s3][:, j * 128:(j + 1) * 128], V_aug[:, kb, h, :],
                   first, (kb == qt),
                   [("PT", s3), ("V_aug", kb, 0), ("V_aug", kb, 1), "V_ones"], ["oacc"],
                   signal=(j == nq - 1), skip_group_check=True)

        for i in range(min(LA, n)):
            qk(i)
            mid(i)
        hooks = list(hooks)
        nh = len(hooks)
        hook_at = {}
        for j, fn in enumerate(hooks):
            hook_at.setdefault(min(n - 1, 1 + (j * max(1, n - 2)) // max(1, nh)), []).append(fn)
        for i in range(n):
            if i + LA < n:
                qk(i + LA)
                mid(i + LA)
            for fn in hook_at.get(i, ()):
                fn()
            pv(i)
        for g, (q_a, q_b) in enumerate(((0, 7), (7, 14), (14, 16))):
            nq = q_b - q_a
            ob = ps[:, (2 + g) * 512:(2 + g) * 512 + nq * 65].rearrange("p (q c) -> p q c", c=65)
            recip(rden[:, q_a:q_b], ob[:, :, 64], ["oacc"], [("rden", g)])
            rb = bass.AP(rden.tensor, rden.offset + q_a, [list(rden.ap[0]), [1, nq], [0, 64]])
            tt_("dve", O_tm[:, q_a:q_b, h * 64:(h + 1) * 64], ob[:, :, 0:64], rb, ALU.mult,
                ["oacc", ("rden", g)], [("O_tm", h, g)])

    def mixer_epilogue(s, wo, layer, O_tm, src_d, dst_d):
        L = ln_tiles_alloc()
        ln_load_params(L, "ln_mix_g", "ln_mix_b", layer)
        for tt in range(NT):
            dst = xTb[:, :, tt * 128:(tt + 1) * 128]
            transpose_tile(O_tm[:, tt, :], 8, "O_tm_all", [("dve" if tt % 2 == 0 else "act", dst, 0, 8,
                                                            [("xTb", tt)])], tt % 2)
        def y_mm(tt):
            yb = 2 + 2 * (tt % 2)
            for half in range(2):
                for k in range(8):
                    mm(bank(yb + half), xTb[:, k, tt * 128:(tt + 1) * 128], wo[:, k, half * 512:(half + 1) * 512],
                       k == 0, k == 7, [("xTb", tt), "wo"], [("ps", yb + half)], signal=(k == 7))

        load_xres(L, src_d, s * SEQ, 0)
        load_xres(L, src_d, s * SEQ + 128, 1)
        y_mm(0)
        for tt in range(NT):
            slot = tt % 2
            if tt + 1 < NT:
                y_mm(tt + 1)
            ln_stage1(L, slot, 2 + 2 * slot)
            if tt + 2 < NT:
                load_xres(L, src_d, s * SEQ + (tt + 2) * 128, slot)
            if tt > 0:
                ln_stage2(L, (tt - 1) % 2, dst_d, s * SEQ + (tt - 1) * 128, tt - 1, True, 6 + (tt - 1) % 2)
        ln_stage2(L, (NT - 1) % 2, dst_d, s * SEQ + (NT - 1) * 128, NT - 1, True, 6 + (NT - 1) % 2)

    def ffn(s, layer, src_d, dst_d, final):
        m0 = A.mark()
        hT = A.alloc([NG, 1024], BF16)
        wout = A.alloc([NG, DM], BF16)
        win = [A.alloc([2, 8, 128], BF16) for _ in range(3)]
        acc = [[A.alloc([1024], F32) for _ in range(2)] for _ in range(2)]
        L = ln_tiles_alloc()
        load_w(wout, wout_d[layer].rearrange("(g p) c -> p g c", p=128), "wout", parts=4)
        ln_load_params(L, "ln_ffn_g", "ln_ffn_b", layer)
        wsrc = win_d[layer]

        def load_win(g):
            slot = g % 3
            pairs = [(win[slot][:, p], wsrc[:, p * DFF + g * 128: p * DFF + (g + 1) * 128]
                      .rearrange("(k q) c -> q k c", q=128)) for p in range(2)]
            dma("pool", pairs, [], [("win", slot)], ("win", slot))

        pu_ctr = 0
        for hf in range(2):
            t0 = hf * 1024
            load_win(0)
            load_win(1)
            for g in range(NG):
                if g + 2 < NG:
                    load_win(g + 2)
                wslot = g % 3
                aslot = g % 2
                for part in range(2):
                    c = part * NG + g
                    pslot = pu_ctr % 4
                    pu_ctr += 1
                    b0 = 2 * pslot
                    pu = ps[:, b0 * 512:b0 * 512 + 1024]
                    a = acc[part][aslot]
                    akey = ("acc", part, aslot)
                    for tc in range(2):
                        for k in range(8):
                            mm(bank(b0 + tc), win[wslot][:, part, k, :],
                               xTb[:, k, t0 + tc * 512:t0 + (tc + 1) * 512], k == 0, k == 7,
                               [("win", wslot)] + xkeys(t0 + tc * 512, t0 + (tc + 1) * 512),
                               [("ps", b0 + tc)], signal=(k == 7))
                    pkeys = [("ps", b0), ("ps", b0 + 1)]
                    w2, w1, w0 = (cw[:, layer, c, j:j + 1] for j in (2, 1, 0))
                    act(a, pu, AF.Identity, pkeys + ["cw", "cb"], [akey], bias=cb[:, layer, c:c + 1], scale=w2)
                    stt(a[:, 1:1024], pu[:, 0:1023], w1, a[:, 1:1024], ALU.mult, ALU.add,
                        pkeys + [akey, "cw"], [akey])
                    stt(a[:, 2:1024], pu[:, 0:1022], w0, a[:, 2:1024], ALU.mult, ALU.add,
                        pkeys + [akey, "cw"], [akey])
                    if hf == 1:
                        stt(a[:, 0:2], halo[:, c, 0:2], w0, a[:, 0:2], ALU.mult, ALU.add,
                            [akey, ("halo", c), "cw"], [akey])
                        stt(a[:, 0:1], halo[:, c, 1:2], w1, a[:, 0:1], ALU.mult, ALU.add,
                            [akey, ("halo", c), "cw"], [akey])
                    else:
                        cp("dve", halo[:, c, :], pu[:, 1022:1024], pkeys, [("halo", c)])
                ag, av = acc[0][aslot], acc[1][aslot]
                act(ag, ag, AF.Silu, [("acc", 0, aslot)], [("acc", 0, aslot)])
                tt_("pool", hT[:, g, :], ag, av, ALU.mult, [("acc", 0, aslot), ("acc", 1, aslot)], [("hT", g)])
            def y_mm2(ti):
                yb = 2 * (ti % 2)
                for half in range(2):
                    for g in range(NG):
                        mm(bank(yb + half), hT[:, g, ti * 128:(ti + 1) * 128],
                           wout[:, g, half * 512:(half + 1) * 512], g == 0, g == NG - 1,
                           [("hT", g), "wout"], [("ps", yb + half)], signal=(g == NG - 1))

            load_xres(L, src_d, s * SEQ + t0, 0)
            load_xres(L, src_d, s * SEQ + t0 + 128, 1)
            y_mm2(0)
            for ti in range(8):
                tt = hf * 8 + ti
                slot = ti % 2
                if ti + 1 < 8:
                    y_mm2(ti + 1)
                ln_stage1(L, slot, 2 * slot)
                if ti + 2 < 8:
                    load_xres(L, src_d, s * SEQ + (tt + 2) * 128, slot)
                if ti > 0:
                    ln_stage2(L, (ti - 1) % 2, dst_d, s * SEQ + (tt - 1) * 128, tt - 1, not final, 4 + (ti - 1) % 2)
            ln_stage2(L, 1, dst_d, s * SEQ + (hf * 8 + 7) * 128, hf * 8 + 7, not final, 5)
        P.barrier()
        A.release(m0)

    def phase_p0(s):
        m = A.mark()
        xt = [A.alloc([DM], F32) for _ in range(2)]
        xb0 = [A.alloc([DM], BF16) for _ in range(2)]
        for tt in range(NT):
            slot = tt % 2
            dma("sp", [(xt[slot], x_d[s * SEQ + tt * 128: s * SEQ + (tt + 1) * 128, :])], [], [("xt", slot)],
                ("xt", slot))
            cp("act", xb0[slot], xt[slot], [("xt", slot)], [("xb0", slot)])
            transpose_tile(xb0[slot], 8, ("xb0", slot),
                           [("dve", xTb[:, :, tt * 128:(tt + 1) * 128], 0, 8, [("xTb", tt)])], tt % 2)
        P.barrier()
        A.release(m)

    def v_evac(bk, V_aug, tt, half):
        src = bank(bk).rearrange("p (h c) -> p h c", c=64)
        dst = V_aug[:, tt, half * 8:(half + 1) * 8, 0:64]
        cp("act" if half == 0 else "dve", dst, src, [("ps", bk), "V_ones"], [("V_aug", tt, half)])

    def phase_a(s):
        m_a = A.mark()
        V_aug = A.alloc([NT, H, 65], BF16)
        O_tm = A.alloc([NT, DM], BF16)
        wv = A.alloc([8, DM], BF16)
        m_a1 = A.mark()
        dtab = A.alloc([SEQ], F32)
        ltab = A.alloc([SEQ], F32)
        T8 = A.alloc([SEQ], F32)
        Th = [[A.alloc([SEQ], BF16) for _ in range(2)] for _ in range(2)]
        QT = [[A.alloc([SEQ], BF16) for _ in range(2)] for _ in range(2)]
        KT = [A.alloc([SEQ], BF16) for _ in range(2)]
        wqk = [A.alloc([2, 8, 128], BF16) for _ in range(2)]
        for sl_ in range(2):
            memset("pool", QT[sl_][0][64:128, :], 0.0, [], [("QTz", sl_, 0)])
            memset("pool", QT[sl_][1][0:64, :], 0.0, [], [("QTz", sl_, 1)])
        Sb = None
        PT = [A.alloc([512], BF16) for _ in range(4)]
        rden = A.alloc([NT], F32)
        load_w(wv, wqkv_d[:, 2 * DM:3 * DM].rearrange("(k p) c -> p k c", p=128), "wo", parts=2)
        ld(dtab, dtab_d, "dtab")
        ld(ltab, ltab_d, "ltab")

        def load_wqk(hp):
            slot = hp % 2
            pairs = [(wqk[slot][:, j], wqkv_d[:, j * DM + hp * 128: j * DM + (hp + 1) * 128]
                      .rearrange("(k q) c -> q k c", q=128)) for j in range(2)]
            dma("pool", pairs, [], [("wqk", slot)], ("wqk", slot))

        load_wqk(0)
        load_wqk(1)
        memset("pool", V_aug[:, :, :, 64:65], 1.0, [], ["V_ones"])
        for tt in range(NT):
            for half in range(2):
                bk = 5 + (2 * tt + half) % 3
                for k in range(8):
                    mm(bank(bk), xTb[:, k, tt * 128:(tt + 1) * 128], wv[:, k, half * 512:(half + 1) * 512],
                       k == 0, k == 7, [("xTb", tt), "wo"], [("ps", bk)], signal=(k == 7))
                v_evac(bk, V_aug, tt, half)
        load_w(wv, awo_d.rearrange("(k p) c -> p k c", p=128), "wo", parts=2)

        pctr = [0]

        def proj_piece(hp, j, tc):
            slot = hp % 2
            bk = 7
            for k in range(8):
                mm(bank(bk), wqk[slot][:, j, k, :], xTb[:, k, tc * 512:(tc + 1) * 512], k == 0, k == 7,
                   [("wqk", slot)] + xkeys(tc * 512, (tc + 1) * 512), [("ps", bk)], signal=(k == 7))
            cs = slice(tc * 512, (tc + 1) * 512)
            if j == 0:
                cp("act", QT[slot][0][0:64, cs], ps[0:64, bk * 512:(bk + 1) * 512], [("ps", bk), ("QTz", slot, 0)],
                   [("QT", slot, 0)])
                cp("act", QT[slot][1][64:128, cs], ps[64:128, bk * 512:(bk + 1) * 512], [("ps", bk), ("QTz", slot, 1)],
                   [("QT", slot, 1)])
            else:
                cp("act", KT[slot][:, cs], bank(bk), [("ps", bk)], [("KT", slot)])

        def build_th(h):
            ths = h % 2
            stt(T8, dtab, -8.0 * alibi_slope(h), ltab, ALU.mult, ALU.add, ["dtab", "ltab"], ["T8"])
            cp("act", Th[ths][0], T8, ["T8"], [("Thi", ths), ("Th", ths)])
            tt_("dve", Th[ths][1], T8, Th[ths][0], ALU.subtract, ["T8", ("Thi", ths)], [("Th", ths)])

        for j in range(2):
            for tc in range(4):
                proj_piece(0, j, tc)
        build_th(0)
        for hp in range(H // 2):
            slot = hp % 2
            pieces = []
            if hp + 1 < H // 2:
                pieces = [(lambda j=j, tc=tc: proj_piece(hp + 1, j, tc)) for j in range(2) for tc in range(4)]
            for hh in range(2):
                h = 2 * hp + hh
                ths = h % 2
                hooks = []
                if h + 1 < H:
                    hooks.append(lambda h=h: build_th(h + 1))
                if hh == 1:
                    hooks += pieces
                    if hp + 2 < H // 2:
                        hooks.append(lambda hp=hp: load_wqk(hp + 2))
                pb = 64 * hh
                attention(h, QT[slot][hh], KT[slot], ("QT", slot, hh), ("KT", slot),
                          0.125, True, Th[ths], ("Th", ths), V_aug, O_tm, Sb, PT, rden, hooks=hooks)
        P.barrier()
        A.release(m_a1)
        if stop_after == "A2":
            if debug_out:
                dma("sp", [(dbg_d.rearrange("(t p) c -> p t c", p=128), O_tm)], [], [], "dbg")
            return
        mixer_epilogue(s, wv, 0, O_tm, x_d, xr[0])
        P.barrier()
        A.release(m_a)

    def phase_c(s):
        m_c = A.mark()
        V_aug = A.alloc([NT, H, 65], BF16)
        O_tm = A.alloc([NT, DM], BF16)
        m_c2 = A.mark()
        ckvT = A.alloc([2, SEQ], BF16)
        cqT = A.alloc([3, SEQ], BF16)
        QT = [A.alloc([SEQ], BF16) for _ in range(2)]
        KT = [A.alloc([SEQ], BF16) for _ in range(2)]
        ropec = A.alloc([SEQ], F32)
        ropes = A.alloc([SEQ], F32)
        t1 = [A.alloc([512], F32) for _ in range(2)]
        t2 = [A.alloc([512], F32)]
        wuk = A.alloc([2, DM], BF16)
        wuq_c = A.alloc([3, H, 128], BF16)
        Sb = None
        PT = [A.alloc([512], BF16) for _ in range(4)]
        rden = A.alloc([NT], F32)
        m_c1 = A.mark()
        wuv = A.alloc([2, DM], BF16)
        ckvn = [A.alloc([256], BF16) for _ in range(2)]
        cqn = [A.alloc([384], BF16) for _ in range(2)]
        junk = A.alloc([384], F32)
        ss = [A.alloc([2], F32) for _ in range(2)]
        sd2 = [A.alloc([2], F32) for _ in range(2)]
        rs2 = [A.alloc([2], F32) for _ in range(2)]
        load_w(wuv, wuv_d.rearrange("(k p) c -> p k c", p=128), "wuv")
        load_w(wuk, wuk_d.rearrange("(k p) c -> p k c", p=128), "wuk")
        wuq_src = wuq_d.rearrange("(k p) (h c) -> p k h c", p=128, c=96)
        dma("pool", [(wuq_c[:, k, :, 0:96], wuq_src[:, k, :, :]) for k in range(3)], [], ["wuq"], "wuq")
        dma("sp", [(ropec[64:96, :], ropec_d)], [], ["ropec"], "ropec")
        dma("sp", [(ropes[64:96, :], ropes_d), (ropes[96:128, :], ropes_d)], [], ["ropes"], "ropes")
        memset("pool", V_aug[:, :, :, 64:65], 1.0, [], ["V_ones"])
        if c1_parts < 1:
            P.barrier()
            return
        for k in range(3):
            act(wuq_c[:, k, :, 96:112], wuq_c[:, k, :, 80:96], AF.Identity, ["wuq"], [("wuq_c1", k)], scale=-1.0)
            act(wuq_c[:, k, :, 112:128], wuq_c[:, k, :, 64:80], AF.Identity, ["wuq", ("wuq_c1", k)],
                [("wuq_rot", k)])
        rotkeys = [("wuq_rot", k) for k in range(3)]
        def c1_stage_a(tt):
            slot = tt % 2
            bA, bB = (0, 1) if slot == 0 else (2, 3)
            for (bk, w_, ncol, wk_) in ((bA, wdkv, 256, "wdkv"), (bB, wdq, 384, "wdq")):
                for k in range(8):
                    mm(bank(bk, ncol), xTb[:, k, tt * 128:(tt + 1) * 128], w_[:, k, :], k == 0, k == 7,
                       [("xTb", tt), wk_], [("ps", bk)], signal=(k == 7))
            act(junk[:, 0:256], bank(bA, 256), AF.Square, [("ps", bA)], ["junk", ("ss0", slot)],
                accum_out=ss[slot][:, 0:1])
            act(junk[:, 0:384], bank(bB, 384), AF.Square, [("ps", bB), "junk"], ["junk", ("ss1", slot)],
                accum_out=ss[slot][:, 1:2])
            act(sd2[slot][:, 0:1], ss[slot][:, 0:1], AF.Sqrt, [("ss0", slot), "epsc"], [("sd0", slot)],
                bias=epsrms, scale=1.0 / 256.0)
            act(sd2[slot][:, 1:2], ss[slot][:, 1:2], AF.Sqrt, [("ss1", slot), "epsc"], [("sd1", slot)],
                bias=epsrms, scale=1.0 / 384.0)
            recip(rs2[slot], sd2[slot], [("sd0", slot), ("sd1", slot)], [("rs2", slot)])
            stt(ckvn[slot], bank(bA, 256), rs2[slot][:, 0:1], gkv_bc, ALU.mult, ALU.mult,
                [("ps", bA), ("rs2", slot), "gkv"], [("ckvn", slot)])
            stt(cqn[slot], bank(bB, 384), rs2[slot][:, 1:2], gq_bc, ALU.mult, ALU.mult,
                [("ps", bB), ("rs2", slot), "gq"], [("cqn", slot)])

        def c1_stage_b(tt):
            slot = tt % 2
            tb = 4 + slot
            pb_ = bank_bf(tb).rearrange("p (c n) -> p c n", n=128)
            for c in range(2):
                tr(pb_[:, c, :], ckvn[slot][:, c * 128:(c + 1) * 128], [("ckvn", slot)], [("ps", tb)], signal=False)
            for c in range(3):
                tr(pb_[:, 2 + c, :], cqn[slot][:, c * 128:(c + 1) * 128], [("cqn", slot)], [("ps", tb)],
                   signal=(c == 2))
            ceng = "dve" if slot == 0 else "act"
            cp(ceng, ckvT[:, :, tt * 128:(tt + 1) * 128], pb_[:, 0:2, :], [("ps", tb)], [("ckvT", tt)])
            cp(ceng, cqT[:, :, tt * 128:(tt + 1) * 128], pb_[:, 2:5, :], [("ps", tb)], [("cqT", tt)])
            for half in range(2):
                bk = 6 + half
                for c in range(2):
                    mm(bank(bk), ckvT[:, c, tt * 128:(tt + 1) * 128], wuv[:, c, half * 512:(half + 1) * 512],
                       c == 0, c == 1, [("ckvT", tt), "wuv"], [("ps", bk)], signal=(c == 1))
                v_evac(bk, V_aug, tt, half)

        if c1_parts < 2:
            P.barrier()
            return
        c1_stage_a(0)
        for tt in range(NT):
            if tt + 1 < NT:
                c1_stage_a(tt + 1)
            c1_stage_b(tt)
        for tc in range(4 if c1_parts >= 5 else 0):
            sl = tc % 2
            bA, bB = (0, 1) if sl == 0 else (2, 3)
            cs = slice(tc * 512, (tc + 1) * 512)
            for (bk, w_, wk_) in ((bA, wkr, "wkr"), (bB, wkr_rot, "wkr_rot")):
                for k in range(8):
                    mm(ps[64:96, bk * 512:(bk + 1) * 512], w_[:, k, :], xTb[:, k, cs], k == 0, k == 7,
                       [wk_] + xkeys(tc * 512, (tc + 1) * 512), [("ps", bk)], signal=(k == 7))
            tt_("dve", t1[sl][64:96, :], ps[64:96, bA * 512:(bA + 1) * 512], ropec[64:96, cs], ALU.mult,
                [("ps", bA), "ropec"], [("t1", sl)])
            tt_("dve", t2[0][64:96, :], ps[64:96, bB * 512:(bB + 1) * 512], ropes[64:96, cs], ALU.mult,
                [("ps", bB), "ropes"], [("t2", 0)])
            tt_("pool", KT[0][64:96, cs], t1[sl][64:96, :], t2[0][64:96, :], ALU.add,
                [("t1", sl), ("t2", 0)], [("KTr", 0, tc)])
            tt_("pool", KT[1][64:96, cs], t1[sl][64:96, :], t2[0][64:96, :], ALU.add,
                [("t1", sl), ("t2", 0)], [("KTr", 1, tc)])
        P.barrier()
        A.release(m_c1)
        if stop_after == "C1":
            if debug_out:
                dma("sp", [(dbg_d.rearrange("(t p) c -> p t c", p=128)[:, :, 0:64], V_aug[:, :, 0, 0:64]),
                           (dbg_d.rearrange("(t p) c -> p t c", p=128)[0:96, :, 128:256], KT[0][0:96, :].rearrange("p (t c) -> p t c", c=128))], [], [], "dbg")
            return
        wo_off = A.mark()
        wo_c = A.alloc([8, DM], BF16)
        load_w(wo_c, bwo_d.rearrange("(k p) c -> p k c", p=128), "wo", parts=2)
        scale1 = 96.0 ** -0.5
        allckv = [("ckvT", t) for t in range(NT)]
        allcq = [("cqT", t) for t in range(NT)]

        def proj_head_piece(h, tc, part):
            slot = h % 2
            cs = slice(tc * 512, (tc + 1) * 512)
            bA, bB = 7, 7
            sl = tc % 2
            if part == 0:
                c_lo, r_lo = (h * 64, 0) if h + 1 < H else ((h - 1) * 64, 64)
                for c in range(2):
                    mm(bank(bB), wuk[:, c, c_lo:c_lo + 128], ckvT[:, c, cs],
                       c == 0, c == 1, ["wuk"] + allckv[tc * 4:(tc + 1) * 4], [("ps", bB)], signal=(c == 1))
                cp("dve", KT[slot][0:64, cs], ps[r_lo:r_lo + 64, bB * 512:(bB + 1) * 512], [("ps", bB)],
                   [("KT", slot)])
            else:
                for c in range(3):
                    mm(bank(bA), wuq_c[:, c, h, :], cqT[:, c, cs],
                       c == 0, c == 2, rotkeys + allcq[tc * 4:(tc + 1) * 4], [("ps", bA)], signal=(c == 2))
                cp("dve", QT[slot][0:64, cs], ps[0:64, bA * 512:(bA + 1) * 512], [("ps", bA)], [("QT", slot)])
                tt_("dve", t1[sl][64:96, :], ps[64:96, bA * 512:(bA + 1) * 512], ropec[64:96, cs], ALU.mult,
                    [("ps", bA), "ropec"], [("t1", sl)])
                tt_("dve", t2[0][64:96, :], ps[96:128, bA * 512:(bA + 1) * 512], ropes[96:128, cs], ALU.mult,
                    [("ps", bA), "ropes"], [("t2", 0)])
                tt_("pool", QT[slot][64:96, cs], t1[sl][64:96, :], t2[0][64:96, :], ALU.add,
                    [("t1", sl), ("t2", 0)], [("QT", slot)])

        for tc in range(4):
            for part in range(2):
                proj_head_piece(0, tc, part)
        for h in range(H):
            slot = h % 2
            hooks = []
            if h + 1 < H:
                hooks = [(lambda h=h, tc=tc, part=part: proj_head_piece(h + 1, tc, part))
                         for tc in range(4) for part in range(2)]
            attention(h, QT[slot][0:96, :], KT[slot][0:96, :], ("QT", slot), ("KT", slot),
                      scale1, False, None, None, V_aug, O_tm, Sb, PT, rden, hooks=hooks)
        P.barrier()
        A.release(m_c2)
        if stop_after == "C2":
            if debug_out:
                dma("sp", [(dbg_d.rearrange("(t p) c -> p t c", p=128), O_tm)], [], [], "dbg")
            return
        mixer_epilogue(s, wo_c, 1, O_tm, xr[1], xr[2])
        assert A.top <= wo_off, "epilogue tiles overlap the prefetched w_o"
        P.barrier()
        A.release(m_c)

    order = ["P0", "A2", "A", "B", "C1", "C2", "C", "D"]
    lim = order.index(stop_after) if stop_after else len(order)
    for s in range(NSEQ if stop_after is None else 1):
        if s > 0:
            P.barrier()
            P.new_engine_sems()
        phase_p0(s)
        if lim == 0:
            break
        if skip_ab:
            phase_c(s)
            break
        phase_a(s)
        if lim <= 2:
            if debug_out and stop_after == "A":
                dma("sp", [(dbg_d.rearrange("(k p) t -> p k t", p=128), xTb)], [], [], "dbg")
            break
        ffn(s, 0, xr[0], xr[1], False)
        if lim == 3:
            if debug_out:
                dma("sp", [(dbg_d.rearrange("(k p) t -> p k t", p=128), xTb)], [], [], "dbg")
            break
        phase_c(s)
        if lim <= 6:
            if debug_out and stop_after == "C":
                pass
            break
        ffn(s, 1, xr[2], y_d, True)

    P.barrier()
    P.emit(block)
    blockc.__exit__(None, None, None)
    psc.__exit__(None, None, None)
    sb.__exit__(None, None, None)
    return nc


def _constants():
    c = {}
    c["c_ident"] = np.eye(128, dtype=np.float32).astype(ml_dtypes.bfloat16)
    ki = np.arange(128)[:, None]
    cc = np.arange(SEQ)[None, :]
    dist = cc - ki
    c["c_dtab"] = dist.astype(np.float32)
    mult = ((dist >= 0) & (dist <= 128)).astype(np.int64)
    mult = mult + ((dist >= 0) & (dist % 4 == 0) & (dist <= 512))
    mult = mult + ((dist >= 0) & (dist % 16 == 0) & (dist <= 2048))
    lt = np.where(mult > 0, 8.0 * np.log(np.maximum(mult, 1).astype(np.float64)), NEG)
    c["c_ltab"] = lt.astype(np.float32)
    j = np.arange(128)[None, :]
    c["c_cmaskb"] = np.where(j >= ki, 0.0, -30000.0).astype(np.float32).astype(ml_dtypes.bfloat16)
    inv_freq = (np.float32(10000.0) ** (-(np.arange(0, 32, 2, dtype=np.float32)) / np.float32(32.0))).astype(np.float32)
    ang = (np.arange(SEQ, dtype=np.float32)[:, None] * inv_freq[None, :]).astype(np.float32)
    cos = np.cos(ang).astype(np.float32).T
    sin = np.sin(ang).astype(np.float32).T
    c["c_ropec"] = np.ascontiguousarray(np.concatenate([cos, cos], axis=0))
    c["c_ropes"] = np.ascontiguousarray(np.concatenate([sin, sin], axis=0))
    return c


def make_in_maps(inputs):
    f = lambda a: np.ascontiguousarray(np.asarray(a, dtype=np.float32))
    x = f(inputs["x"]).reshape(NCORES, NSEQ * SEQ, DM)
    shared = {
        "a_w_qkv": f(inputs["a_w_qkv"])[0],
        "a_w_o": f(inputs["a_w_o"])[0],
        "kv_w_dkv": f(inputs["kv_w_dkv"]),
        "kv_norm_g": f(inputs["kv_norm_g"]).reshape(1, 256),
        "kv_w_kr": f(inputs["kv_w_kr"]),
        "kv_w_uk": f(inputs["kv_w_uk"]),
        "kv_w_uv": f(inputs["kv_w_uv"]),
        "b_w_dq": f(inputs["b_w_dq"])[0],
        "b_q_norm_g": f(inputs["b_q_norm_g"]).reshape(1, 384),
        "b_w_uq": f(inputs["b_w_uq"])[0],
        "b_w_o": f(inputs["b_w_o"])[0],
        "ffn_w_in": f(inputs["ffn_w_in"]),
        "ffn_w_out": f(inputs["ffn_w_out"]),
        "ln_mix_g": f(inputs["ln_mix_g"]),
        "ln_mix_b": f(inputs["ln_mix_b"]),
        "ln_ffn_g": f(inputs["ln_ffn_g"]),
        "ln_ffn_b": f(inputs["ln_ffn_b"]),
    }
    cwv = f(inputs["ffn_conv_w"]).reshape(2, 3, 2 * NG, 128)
    shared["ffn_conv_w"] = np.ascontiguousarray(cwv.transpose(3, 0, 2, 1)).reshape(128, -1)
    cbv = f(inputs["ffn_conv_b"]).reshape(2, 2 * NG, 128)
    shared["ffn_conv_b"] = np.ascontiguousarray(cbv.transpose(2, 0, 1)).reshape(128, -1)
    shared.update(_constants())
    return [dict(shared, x=np.ascontiguousarray(x[i])) for i in range(NCORES)]


_NC_CACHE = {}


def kernel(**inputs):
    in_maps = make_in_maps(inputs)
    if "nc" not in _NC_CACHE:
        _NC_CACHE["nc"] = build_program()
    nc = _NC_CACHE["nc"]
    res = run_bass_kernel_spmd(nc, in_maps, core_ids=list(range(NCORES)))
    out = np.stack([np.asarray(r["y"], dtype=np.float32) for r in res.results], axis=0)
    return out.reshape(NCORES * NSEQ, SEQ, DM)
```

```python
import math
import numpy as np
import ml_dtypes
import concourse.bass as bass
import concourse.mybir as mybir
from concourse.bass_utils import run_bass_kernel_spmd

F32 = mybir.dt.float32
BF16 = mybir.dt.bfloat16
AF = mybir.ActivationFunctionType
ALU = mybir.AluOpType

NCORES = 8
SEQ = 2048
DM = 1024
NSEQ = 2
NT = SEQ // 128
H = 16
DFF = 2816
NG = DFF // 128
ALPHA = 4.0 ** 0.25
LN_EPS = 1e-5
RMS_EPS = 1e-6
NEG = -1.0e9


class Prog:
    ENG = ("pe", "act", "dve", "pool", "sp")

    def __init__(self, nc):
        self.nc = nc
        self.q = {e: [] for e in self.ENG}
        self.esem = {}
        self.ecnt = {}
        self.waited = {e: {} for e in self.ENG}
        self.res = {}
        self.dsem = {}
        self.pending = {e: [] for e in self.ENG}
        self.last = {e: None for e in self.ENG}
        self.gen = 0
        self.new_engine_sems()

    def new_engine_sems(self):
        for e in self.ENG:
            if e == "sp":
                continue
            self.esem[e] = self.nc.alloc_semaphore(name=f"es_{e}_{self.gen}")
            self.ecnt[e] = 0
        self.gen += 1

    def _waits(self, eng, deps):
        waits = {}
        for tok in deps:
            sem, val, teng = tok
            if eng == "pe" and teng == "pe":
                continue
            assert val is not None, "dependency on unresolved (non-signalled) op"
            k = id(sem)
            if self.waited[eng].get(k, 0) >= val:
                continue
            if k not in waits or waits[k][1] < val:
                waits[k] = (sem, val)
        for k, (s, v) in waits.items():
            self.waited[eng][k] = v
        return list(waits.values())

    def op(self, eng, fn, reads=(), writes=(), signal=True, dma=None):
        deps = []
        for r in reads:
            st = self.res.get(r)
            if st is not None and st[0] is not None:
                deps.append(st[0])
        for w in writes:
            st = self.res.get(w)
            if st is not None:
                if st[0] is not None:
                    deps.append(st[0])
                deps.extend(st[1])
        waits = self._waits(eng, deps)
        if dma is not None:
            key, n = dma
            ent = self.dsem.get(key)
            if ent is None:
                ent = [self.nc.alloc_semaphore(name=f"ds_{len(self.dsem)}"), 0]
                self.dsem[key] = ent
            ent[1] += 16 * n
            tok = [ent[0], ent[1], "dma"]
            inc = (ent[0], 16)
        elif signal:
            self.ecnt[eng] += 1
            tok = [self.esem[eng], self.ecnt[eng], eng]
            inc = (self.esem[eng], 1)
            for p in self.pending[eng]:
                p[0], p[1], p[2] = tok[0], tok[1], tok[2]
            self.pending[eng] = []
            self.last[eng] = tok
        else:
            tok = [None, None, eng]
            inc = None
            self.pending[eng].append(tok)
        self.q[eng].append((waits, fn, inc))
        for r in reads:
            self.res.setdefault(r, [None, []])[1].append(tok)
        for w in writes:
            self.res[w] = [tok, []]
        return tok

    def barrier(self):
        for e in self.ENG:
            assert not self.pending[e], f"pending unsignalled ops on {e} at barrier"
        toks = [t for t in self.last.values() if t is not None]
        toks += [[ent[0], ent[1], "dma"] for ent in self.dsem.values() if ent[1] > 0]
        for e in self.ENG:
            w = self._waits(e, toks)
            if w:
                self.q[e].append((w, None, None))
        self.res = {}

    def emit(self, block):
        m = {"pe": block.tensor, "act": block.scalar, "dve": block.vector,
             "pool": block.gpsimd, "sp": block.sync}
        for e in self.ENG:
            def body(eng, e=e):
                for waits, fn, inc in self.q[e]:
                    for (s, v) in waits:
                        eng.wait_ge(s, v)
                    if fn is None:
                        continue
                    ins = fn(eng)
                    if inc is not None:
                        if isinstance(ins, (list, tuple)):
                            for i in ins:
                                i.then_inc(inc[0], inc[1])
                        else:
                            ins.then_inc(inc[0], inc[1])
            m[e](body)


class Arena:
    def __init__(self, base, words):
        self.base = base
        self.words = words
        self.top = 0

    def mark(self):
        return self.top

    def release(self, m):
        self.top = m

    def alloc(self, shape, dtype):
        n = 1
        for s in shape:
            n *= s
        nbytes = n * (2 if dtype == BF16 else 4)
        w = (nbytes + 3) // 4
        w = (w + 7) // 8 * 8
        assert self.top + w <= self.words, f"SBUF arena overflow: need {self.top + w} have {self.words}"
        ap = self.base[:, self.top:self.top + w]
        self.top += w
        if dtype == BF16:
            ap = ap.bitcast(BF16)
        ap = ap[:, 0:n]
        if len(shape) == 1:
            return ap
        names = " ".join(f"d{i}" for i in range(len(shape)))
        kw = {f"d{i}": shape[i] for i in range(1, len(shape))}
        return ap.rearrange(f"p ({names}) -> p {names}", **kw)


def alibi_slope(h):
    return 2.0 ** (-8.0 * (h + 1) / H)


def build_program(stop_after=None, debug_out=False, skip_ab=False, c1_parts=5):
    nc = bass.Bass("TRN2", target_bir_lowering=False)
    dt = lambda name, shape, dtype=F32, kind="ExternalInput": nc.dram_tensor(name, shape, dtype, kind=kind)
    x_d = dt("x", [NSEQ * SEQ, DM]).ap()
    wqkv_d = dt("a_w_qkv", [DM, 3 * DM]).ap()
    awo_d = dt("a_w_o", [DM, DM]).ap()
    wdkv_d = dt("kv_w_dkv", [DM, 256]).ap()
    gkv_t = dt("kv_norm_g", [1, 256])
    wkr_d = dt("kv_w_kr", [DM, 32]).ap()
    wuk_d = dt("kv_w_uk", [256, DM]).ap()
    wuv_d = dt("kv_w_uv", [256, DM]).ap()
    wdq_d = dt("b_w_dq", [DM, 384]).ap()
    gq_t = dt("b_q_norm_g", [1, 384])
    wuq_d = dt("b_w_uq", [384, 1536]).ap()
    bwo_d = dt("b_w_o", [DM, DM]).ap()
    win_d = dt("ffn_w_in", [2, DM, 2 * DFF]).ap()
    cw_d = dt("ffn_conv_w", [128, 2 * 2 * NG * 3]).ap()
    cb_d = dt("ffn_conv_b", [128, 2 * 2 * NG]).ap()
    wout_d = dt("ffn_w_out", [2, DFF, DM]).ap()
    lng_t = {}
    for nm in ("ln_mix_g", "ln_mix_b", "ln_ffn_g", "ln_ffn_b"):
        lng_t[nm] = dt(nm, [2, DM])
    ident_d = dt("c_ident", [128, 128], BF16).ap()
    dtab_d = dt("c_dtab", [128, SEQ]).ap()
    ltab_d = dt("c_ltab", [128, SEQ]).ap()
    cmaskb_d = dt("c_cmaskb", [128, 128], BF16).ap()
    ropec_d = dt("c_ropec", [32, SEQ]).ap()
    ropes_d = dt("c_ropes", [32, SEQ]).ap()
    y_d = dt("y", [NSEQ * SEQ, DM], F32, "ExternalOutput").ap()
    xr = [dt(f"xr{i}", [NSEQ * SEQ, DM], F32, "Internal").ap() for i in range(3)]
    dbg_d = dt("dbg", [SEQ, DM], BF16, "ExternalOutput").ap() if debug_out else None

    def bcast_rows(t, row, n):
        return bass.AP(t, row * n, [(0, 128), (1, n)])

    WORDS = 53200
    sb = nc.sbuf_tensor("arena", [128, WORDS], F32)
    sbuf = sb.__enter__()
    psc = nc.psum_tensor("psum", [128, 4096], F32)
    ps = psc.__enter__()
    blockc = nc.Block()
    block = blockc.__enter__()

    P = Prog(nc)
    A = Arena(sbuf, WORDS)

    def bank(b, n=512):
        return ps[:, b * 512:b * 512 + n]

    def bank_bf(b):
        return ps[:, b * 512:(b + 1) * 512].bitcast(BF16)

    ident = A.alloc([128], BF16)
    cw = A.alloc([2, 2 * NG, 3], F32)
    cb = A.alloc([2, 2 * NG], F32)
    gkv_bc = A.alloc([256], F32)
    gq_bc = A.alloc([384], F32)
    cmaskb = A.alloc([128], BF16)
    halo = A.alloc([2 * NG, 2], F32)
    xTb = A.alloc([8, SEQ], BF16)
    wdkv = A.alloc([8, 256], BF16)
    wdq = A.alloc([8, 384], BF16)
    wkr = A.alloc([8, 32], BF16)
    wkr_rot = A.alloc([8, 32], BF16)

    def ld(dst, src, key, eng="sp", n=1):
        P.op(eng, lambda e: e.dma_start(out=dst, in_=src), writes=[key], dma=(key, 1))

    ld(ident, ident_d, "ident")
    ld(cw.rearrange("p l c j -> p (l c j)"), cw_d, "cw")
    ld(cb.rearrange("p l c -> p (l c)"), cb_d, "cb")
    ld(gkv_bc, bcast_rows(gkv_t, 0, 256), "gkv")
    ld(gq_bc, bcast_rows(gq_t, 0, 384), "gq")
    ld(cmaskb, cmaskb_d, "cmaskb")

    def xkeys(t0, t1):
        return [("xTb", t) for t in range(t0 // 128, (t1 + 127) // 128)]

    def mm(out, lhsT, rhs, start, stop, reads, writes, signal=True, **kw):
        P.op("pe", lambda e: e.matmul(out, lhsT=lhsT, rhs=rhs, start=start, stop=stop, **kw),
             reads=reads, writes=writes, signal=signal)

    def tr(out, in_, reads, writes, signal=True):
        P.op("pe", lambda e: e.transpose(out=out, in_=in_, identity=ident), reads=list(reads) + ["ident"],
             writes=writes, signal=signal)

    def act(out, in_, func, reads, writes, **kw):
        P.op("act", lambda e: e.activation(out=out, in_=in_, func=func, **kw), reads=reads, writes=writes)

    def stt(out, in0, scalar, in1, op0, op1, reads, writes):
        P.op("dve", lambda e: e.scalar_tensor_tensor(out=out, in0=in0, scalar=scalar, in1=in1, op0=op0, op1=op1),
             reads=reads, writes=writes)

    def tt_(eng, out, in0, in1, op, reads, writes):
        P.op(eng, lambda e: e.tensor_tensor(out=out, in0=in0, in1=in1, op=op), reads=reads, writes=writes)

    def ts_(eng, out, in0, s1, s2, op0, op1, reads, writes):
        if op1 is None:
            P.op(eng, lambda e: e.tensor_scalar(out=out, in0=in0, scalar1=s1, scalar2=None, op0=op0),
                 reads=reads, writes=writes)
        else:
            P.op(eng, lambda e: e.tensor_scalar(out=out, in0=in0, scalar1=s1, scalar2=s2, op0=op0, op1=op1),
                 reads=reads, writes=writes)

    def cp(eng, out, in_, reads, writes):
        if eng == "act":
            act(out, in_, AF.Copy, reads, writes)
        else:
            P.op(eng, lambda e: e.tensor_copy(out=out, in_=in_), reads=reads, writes=writes)

    def recip(out, in_, reads, writes):
        P.op("dve", lambda e: e.reciprocal(out=out, in_=in_), reads=reads, writes=writes)

    def memset(eng, ap, val, reads, writes):
        P.op(eng, lambda e: e.memset(ap, val), reads=reads, writes=writes)

    def dma(eng, pairs, reads, writes, key):
        P.op(eng, lambda e: [e.dma_start(out=o, in_=i) for (o, i) in pairs], reads=reads, writes=writes,
             dma=(key, len(pairs)))

    def load_w(dst, src, key, parts=1):
        k = dst.shape[1]
        step = (k + parts - 1) // parts
        pairs = [(dst[:, a:min(k, a + step)], src[:, a:min(k, a + step)]) for a in range(0, k, step)]
        dma("pool", pairs, [], [key], key)

    def transpose_tile(src, nch, src_key, dsts, bank_id):
        pb = bank_bf(bank_id).rearrange("p (c n) -> p c n", n=128)
        for c in range(nch):
            tr(pb[:, c, :], src[:, c * 128:(c + 1) * 128], [src_key], [("ps", bank_id)], signal=(c == nch - 1))
        for (eng, dst, c0, n, wkeys) in dsts:
            cp(eng, dst, pb[:, c0:c0 + n, :], [("ps", bank_id)], wkeys)

    epsc = A.alloc([4], F32)
    epsln = epsc[:, 0:1]
    epsrms = epsc[:, 1:2]
    memset("pool", epsc[:, 0:1], LN_EPS, [], ["epsc0"])
    memset("pool", epsc[:, 1:2], RMS_EPS, ["epsc0"], ["epsc"])
    load_w(wdkv, wdkv_d.rearrange("(k p) c -> p k c", p=128), "wdkv")
    load_w(wdq, wdq_d.rearrange("(k p) c -> p k c", p=128), "wdq")
    load_w(wkr, wkr_d.rearrange("(k p) c -> p k c", p=128), "wkr")
    act(wkr_rot[:, :, 0:16], wkr[:, :, 16:32], AF.Identity, ["wkr"], ["wkr_rot_a"], scale=-1.0)
    act(wkr_rot[:, :, 16:32], wkr[:, :, 0:16], AF.Identity, ["wkr", "wkr_rot_a"], ["wkr_rot"])

    def ln_tiles_alloc():
        d = {}
        d["xres"] = [A.alloc([DM], F32) for _ in range(2)]
        d["z"] = [A.alloc([DM], F32) for _ in range(2)]
        d["xo"] = [A.alloc([DM], F32) for _ in range(2)]
        d["xb"] = [A.alloc([DM], BF16) for _ in range(2)]
        d["st"] = [A.alloc([2, 6], F32) for _ in range(2)]
        d["mv"] = [A.alloc([2], F32) for _ in range(2)]
        d["sd"] = [A.alloc([1], F32) for _ in range(2)]
        d["rstd"] = [A.alloc([1], F32) for _ in range(2)]
        d["g"] = A.alloc([DM], F32)
        d["b"] = A.alloc([DM], F32)
        return d

    def ln_load_params(L, gname, bname, layer):
        ld(L["g"], bcast_rows(lng_t[gname], layer, DM), "ln_g")
        ld(L["b"], bcast_rows(lng_t[bname], layer, DM), "ln_b")

    def load_xres(L, src_d, row0, slot):
        dma("sp", [(L["xres"][slot], src_d[row0:row0 + 128, :])], [], [("xres", slot)], ("xres", slot))

    def ln_stage1(L, slot, ybank):
        yps = ps[:, ybank * 512:ybank * 512 + 1024]
        xres, z = L["xres"][slot], L["z"][slot]
        st, mv, sd = L["st"][slot], L["mv"][slot], L["sd"][slot]
        stt(z, xres, ALPHA, yps, ALU.mult, ALU.add,
            [("xres", slot), ("ps", ybank), ("ps", ybank + 1)], [("z", slot)])
        P.op("dve", lambda e: e.bn_stats(out=st[:, 0, :], in_=z[:, 0:512]), reads=[("z", slot)],
             writes=[("st0", slot)])
        P.op("dve", lambda e: e.bn_stats(out=st[:, 1, :], in_=z[:, 512:1024]), reads=[("z", slot)],
             writes=[("st1", slot)])
        P.op("dve", lambda e: e.bn_aggr(out=mv, in_=st.rearrange("p a b -> p (a b)")),
             reads=[("st0", slot), ("st1", slot)], writes=[("mv", slot)])
        act(sd, mv[:, 1:2], AF.Sqrt, [("mv", slot), "epsc"], [("sd", slot)], bias=epsln, scale=1.0)

    def ln_stage2(L, slot, dst_d, row0, tt, do_transpose, tbank):
        z, xo, xb = L["z"][slot], L["xo"][slot], L["xb"][slot]
        mv, sd, rstd = L["mv"][slot], L["sd"][slot], L["rstd"][slot]
        recip(rstd, sd, [("sd", slot)], [("rstd", slot)])
        stt(z, z, mv[:, 0:1], L["g"], ALU.subtract, ALU.mult,
            [("z", slot), ("mv", slot), "ln_g"], [("z", slot)])
        stt(xo, z, rstd, L["b"], ALU.mult, ALU.add,
            [("z", slot), ("rstd", slot), "ln_b"], [("xo", slot)])
        dma("pool", [(dst_d[row0:row0 + 128, :], xo)], [("xo", slot)], [], ("xo_st", slot))
        if do_transpose:
            cp("act", xb, xo, [("xo", slot)], [("xb", slot)])
            dst = xTb[:, :, tt * 128:(tt + 1) * 128]
            transpose_tile(xb, 8, ("xb", slot), [("act", dst, 0, 8, [("xTb", tt)])], tbank)

    def oacc(qt):
        b = 2 + qt // 7
        c = (qt % 7) * 65
        return ps[:, b * 512 + c: b * 512 + c + 65]

    SBANKS0 = (0, 1, 5, 6)
    SBANKS1 = (0, 1, 5, 6)
    SKIP_T = 60.0

    def attention(h, QT, KT, qkey, kkey, scale, layer0, Th, thkey, V_aug, O_tm, Sb, PT, rden, hooks=()):
        chunks = []
        slope = alibi_slope(h)
        for kb in range(NT):
            for cc in range(kb // 4, 4):
                q0 = max(128 * kb, 512 * cc)
                N = 512 * (cc + 1) - q0
                if layer0:
                    while N > 0 and slope * (q0 + N - 128 - 128 * kb - 127) > SKIP_T:
                        N -= 128
                    if N <= 0:
                        continue
                chunks.append((kb, q0, N))
        n = len(chunks)
        started = set()
        SBANKS = SBANKS0 if layer0 else SBANKS1
        LA = len(SBANKS) - 1
        NS = len(SBANKS)

        def qk(i):
            kb, q0, N = chunks[i]
            sbk = SBANKS[i % NS]
            if (not layer0) and q0 == 128 * kb:
                mm(bank(sbk, N), KT[:, kb * 128:(kb + 1) * 128], QT[:, q0:q0 + N], True, False,
                   [qkey, kkey], [("ps", sbk)], signal=False)
                mm(bank(sbk, 128), ident, cmaskb, False, True, ["ident", "cmaskb"], [("ps", sbk)])
            elif layer0 and i % 2 == 1:
                mm(bank(sbk, N), KT[:, kb * 128:(kb + 1) * 128], QT[:, q0:q0 + N], True, True,
                   [qkey, kkey], [("ps", sbk)])
            elif layer0:
                c0 = q0 - 128 * kb
                mm(bank(sbk, N), KT[:, kb * 128:(kb + 1) * 128], QT[:, q0:q0 + N], True, False,
                   [qkey, kkey], [("ps", sbk)], signal=False)
                mm(bank(sbk, N), ident, Th[0][:, c0:c0 + N], False, False, ["ident", thkey], [("ps", sbk)],
                   signal=False)
                mm(bank(sbk, N), ident, Th[1][:, c0:c0 + N], False, True, ["ident", thkey], [("ps", sbk)])
            else:
                mm(bank(sbk, N), KT[:, kb * 128:(kb + 1) * 128], QT[:, q0:q0 + N], True, True,
                   [qkey, kkey], [("ps", sbk)])

        def mid(i):
            kb, q0, N = chunks[i]
            sbk = SBANKS[i % NS]
            s3 = i % NS
            S = bank(sbk, N)
            if layer0:
                if i % 2 == 1:
                    c0 = q0 - 128 * kb
                    tt_("dve", S, S, Th[0][:, c0:c0 + N], ALU.add, [("ps", sbk), thkey], [("ps", sbk)])
                    tt_("dve", S, S, Th[1][:, c0:c0 + N], ALU.add, [("ps", sbk), thkey], [("ps", sbk)])
                act(PT[s3][:, 0:N], S, AF.Exp, [("ps", sbk)], [("PT", s3)], scale=scale)
            else:
                act(PT[s3][:, 0:N], S, AF.Exp, [("ps", sbk)], [("PT", s3)], scale=scale)

        def pv(i):
            kb, q0, N = chunks[i]
            s3 = i % NS
            nq = N // 128
            for j in range(nq):
                qt = q0 // 128 + j
                ob = 2 + qt // 7
                first = ob not in started
                started.add(ob)
                mm(oacc(qt), PT[s3][:, j * 128:(j + 1) * 128], V_aug[:, kb, h, :],
                   first, (kb == qt),
                   [("PT", s3), ("V_aug", kb, 0), ("V_aug", kb, 1), "V_ones"], ["oacc"],
                   signal=(j == nq - 1), skip_group_check=True)

        for i in range(min(LA, n)):
            qk(i)
            mid(i)
        hooks = list(hooks)
        nh = len(hooks)
        hook_at = {}
        for j, fn in enumerate(hooks):
            hook_at.setdefault(min(n - 1, 1 + (j * max(1, n - 2)) // max(1, nh)), []).append(fn)
        for i in range(n):
            if i + LA < n:
                qk(i + LA)
                mid(i + LA)
            for fn in hook_at.get(i, ()):
                fn()
            pv(i)
        for g, (q_a, q_b) in enumerate(((0, 7), (7, 14), (14, 16))):
            nq = q_b - q_a
            ob = ps[:, (2 + g) * 512:(2 + g) * 512 + nq * 65].rearrange("p (q c) -> p q c", c=65)
            recip(rden[:, q_a:q_b], ob[:, :, 64], ["oacc"], [("rden", g)])
            rb = bass.AP(rden.tensor, rden.offset + q_a, [list(rden.ap[0]), [1, nq], [0, 64]])
            tt_("dve", O_tm[:, q_a:q_b, h * 64:(h + 1) * 64], ob[:, :, 0:64], rb, ALU.mult,
                ["oacc", ("rden", g)], [("O_tm", h, g)])

    def mixer_epilogue(s, wo, layer, O_tm, src_d, dst_d):
        L = ln_tiles_alloc()
        ln_load_params(L, "ln_mix_g", "ln_mix_b", layer)
        for tt in range(NT):
            dst = xTb[:, :, tt * 128:(tt + 1) * 128]
            transpose_tile(O_tm[:, tt, :], 8, "O_tm_all", [("dve" if tt % 2 == 0 else "act", dst, 0, 8,
                                                            [("xTb", tt)])], tt % 2)
        def y_mm(tt):
            yb = 2 + 2 * (tt % 2)
            for half in range(2):
                for k in range(8):
                    mm(bank(yb + half), xTb[:, k, tt * 128:(tt + 1) * 128], wo[:, k, half * 512:(half + 1) * 512],
                       k == 0, k == 7, [("xTb", tt), "wo"], [("ps", yb + half)], signal=(k == 7))

        load_xres(L, src_d, s * SEQ, 0)
        load_xres(L, src_d, s * SEQ + 128, 1)
        y_mm(0)
        for tt in range(NT):
            slot = tt % 2
            if tt + 1 < NT:
                y_mm(tt + 1)
            ln_stage1(L, slot, 2 + 2 * slot)
            if tt + 2 < NT:
                load_xres(L, src_d, s * SEQ + (tt + 2) * 128, slot)
            if tt > 0:
                ln_stage2(L, (tt - 1) % 2, dst_d, s * SEQ + (tt - 1) * 128, tt - 1, True, 6 + (tt - 1) % 2)
        ln_stage2(L, (NT - 1) % 2, dst_d, s * SEQ + (NT - 1) * 128, NT - 1, True, 6 + (NT - 1) % 2)

    def ffn(s, layer, src_d, dst_d, final):
        m0 = A.mark()
        hT = A.alloc([NG, 1024], BF16)
        wout = A.alloc([NG, DM], BF16)
        win = [A.alloc([2, 8, 128], BF16) for _ in range(3)]
        acc = [[A.alloc([1024], F32) for _ in range(2)] for _ in range(2)]
        L = ln_tiles_alloc()
        load_w(wout, wout_d[layer].rearrange("(g p) c -> p g c", p=128), "wout", parts=4)
        ln_load_params(L, "ln_ffn_g", "ln_ffn_b", layer)
        wsrc = win_d[layer]

        def load_win(g):
            slot = g % 3
            pairs = [(win[slot][:, p], wsrc[:, p * DFF + g * 128: p * DFF + (g + 1) * 128]
                      .rearrange("(k q) c -> q k c", q=128)) for p in range(2)]
            dma("pool", pairs, [], [("win", slot)], ("win", slot))

        pu_ctr = 0
        for hf in range(2):
            t0 = hf * 1024
            load_win(0)
            load_win(1)
            for g in range(NG):
                if g + 2 < NG:
                    load_win(g + 2)
                wslot = g % 3
                aslot = g % 2
                for part in range(2):
                    c = part * NG + g
                    pslot = pu_ctr % 4
                    pu_ctr += 1
                    b0 = 2 * pslot
                    pu = ps[:, b0 * 512:b0 * 512 + 1024]
                    a = acc[part][aslot]
                    akey = ("acc", part, aslot)
                    for tc in range(2):
                        for k in range(8):
                            mm(bank(b0 + tc), win[wslot][:, part, k, :],
                               xTb[:, k, t0 + tc * 512:t0 + (tc + 1) * 512], k == 0, k == 7,
                               [("win", wslot)] + xkeys(t0 + tc * 512, t0 + (tc + 1) * 512),
                               [("ps", b0 + tc)], signal=(k == 7))
                    pkeys = [("ps", b0), ("ps", b0 + 1)]
                    w2, w1, w0 = (cw[:, layer, c, j:j + 1] for j in (2, 1, 0))
                    act(a, pu, AF.Identity, pkeys + ["cw", "cb"], [akey], bias=cb[:, layer, c:c + 1], scale=w2)
                    stt(a[:, 1:1024], pu[:, 0:1023], w1, a[:, 1:1024], ALU.mult, ALU.add,
                        pkeys + [akey, "cw"], [akey])
                    stt(a[:, 2:1024], pu[:, 0:1022], w0, a[:, 2:1024], ALU.mult, ALU.add,
                        pkeys + [akey, "cw"], [akey])
                    if hf == 1:
                        stt(a[:, 0:2], halo[:, c, 0:2], w0, a[:, 0:2], ALU.mult, ALU.add,
                            [akey, ("halo", c), "cw"], [akey])
                        stt(a[:, 0:1], halo[:, c, 1:2], w1, a[:, 0:1], ALU.mult, ALU.add,
                            [akey, ("halo", c), "cw"], [akey])
                    else:
                        cp("dve", halo[:, c, :], pu[:, 1022:1024], pkeys, [("halo", c)])
                ag, av = acc[0][aslot], acc[1][aslot]
                act(ag, ag, AF.Silu, [("acc", 0, aslot)], [("acc", 0, aslot)])
                tt_("pool", hT[:, g, :], ag, av, ALU.mult, [("acc", 0, aslot), ("acc", 1, aslot)], [("hT", g)])
            def y_mm2(ti):
                yb = 2 * (ti % 2)
                for half in range(2):
                    for g in range(NG):
                        mm(bank(yb + half), hT[:, g, ti * 128:(ti + 1) * 128],
                           wout[:, g, half * 512:(half + 1) * 512], g == 0, g == NG - 1,
                           [("hT", g), "wout"], [("ps", yb + half)], signal=(g == NG - 1))

            load_xres(L, src_d, s * SEQ + t0, 0)
            load_xres(L, src_d, s * SEQ + t0 + 128, 1)
            y_mm2(0)
            for ti in range(8):
                tt = hf * 8 + ti
                slot = ti % 2
                if ti + 1 < 8:
                    y_mm2(ti + 1)
                ln_stage1(L, slot, 2 * slot)
                if ti + 2 < 8:
                    load_xres(L, src_d, s * SEQ + (tt + 2) * 128, slot)
                if ti > 0:
                    ln_stage2(L, (ti - 1) % 2, dst_d, s * SEQ + (tt - 1) * 128, tt - 1, not final, 4 + (ti - 1) % 2)
            ln_stage2(L, 1, dst_d, s * SEQ + (hf * 8 + 7) * 128, hf * 8 + 7, not final, 5)
        P.barrier()
        A.release(m0)

    def phase_p0(s):
        m = A.mark()
        xt = [A.alloc([DM], F32) for _ in range(2)]
        xb0 = [A.alloc([DM], BF16) for _ in range(2)]
        for tt in range(NT):
            slot = tt % 2
            dma("sp", [(xt[slot], x_d[s * SEQ + tt * 128: s * SEQ + (tt + 1) * 128, :])], [], [("xt", slot)],
                ("xt", slot))
            cp("act", xb0[slot], xt[slot], [("xt", slot)], [("xb0", slot)])
            transpose_tile(xb0[slot], 8, ("xb0", slot),
                           [("dve", xTb[:, :, tt * 128:(tt + 1) * 128], 0, 8, [("xTb", tt)])], tt % 2)
        P.barrier()
        A.release(m)

    def v_evac(bk, V_aug, tt, half):
        src = bank(bk).rearrange("p (h c) -> p h c", c=64)
        dst = V_aug[:, tt, half * 8:(half + 1) * 8, 0:64]
        cp("act" if half == 0 else "dve", dst, src, [("ps", bk), "V_ones"], [("V_aug", tt, half)])

    def phase_a(s):
        m_a = A.mark()
        V_aug = A.alloc([NT, H, 65], BF16)
        O_tm = A.alloc([NT, DM], BF16)
        wv = A.alloc([8, DM], BF16)
        m_a1 = A.mark()
        dtab = A.alloc([SEQ], F32)
        ltab = A.alloc([SEQ], F32)
        T8 = A.alloc([SEQ], F32)
        Th = [[A.alloc([SEQ], BF16) for _ in range(2)] for _ in range(2)]
        QT = [[A.alloc([SEQ], BF16) for _ in range(2)] for _ in range(2)]
        KT = [A.alloc([SEQ], BF16) for _ in range(2)]
        wqk = [A.alloc([2, 8, 128], BF16) for _ in range(2)]
        for sl_ in range(2):
            memset("pool", QT[sl_][0][64:128, :], 0.0, [], [("QTz", sl_, 0)])
            memset("pool", QT[sl_][1][0:64, :], 0.0, [], [("QTz", sl_, 1)])
        Sb = None
        PT = [A.alloc([512], BF16) for _ in range(4)]
        rden = A.alloc([NT], F32)
        load_w(wv, wqkv_d[:, 2 * DM:3 * DM].rearrange("(k p) c -> p k c", p=128), "wo", parts=2)
        ld(dtab, dtab_d, "dtab")
        ld(ltab, ltab_d, "ltab")

        def load_wqk(hp):
            slot = hp % 2
            pairs = [(wqk[slot][:, j], wqkv_d[:, j * DM + hp * 128: j * DM + (hp + 1) * 128]
                      .rearrange("(k q) c -> q k c", q=128)) for j in range(2)]
            dma("pool", pairs, [], [("wqk", slot)], ("wqk", slot))

        load_wqk(0)
        load_wqk(1)
        memset("pool", V_aug[:, :, :, 64:65], 1.0, [], ["V_ones"])
        for tt in range(NT):
            for half in range(2):
                bk = 5 + (2 * tt + half) % 3
                for k in range(8):
                    mm(bank(bk), xTb[:, k, tt * 128:(tt + 1) * 128], wv[:, k, half * 512:(half + 1) * 512],
                       k == 0, k == 7, [("xTb", tt), "wo"], [("ps", bk)], signal=(k == 7))
                v_evac(bk, V_aug, tt, half)
        load_w(wv, awo_d.rearrange("(k p) c -> p k c", p=128), "wo", parts=2)

        pctr = [0]

        def proj_piece(hp, j, tc):
            slot = hp % 2
            bk = 7
            for k in range(8):
                mm(bank(bk), wqk[slot][:, j, k, :], xTb[:, k, tc * 512:(tc + 1) * 512], k == 0, k == 7,
                   [("wqk", slot)] + xkeys(tc * 512, (tc + 1) * 512), [("ps", bk)], signal=(k == 7))
            cs = slice(tc * 512, (tc + 1) * 512)
            if j == 0:
                cp("act", QT[slot][0][0:64, cs], ps[0:64, bk * 512:(bk + 1) * 512], [("ps", bk), ("QTz", slot, 0)],
                   [("QT", slot, 0)])
                cp("act", QT[slot][1][64:128, cs], ps[64:128, bk * 512:(bk + 1) * 512], [("ps", bk), ("QTz", slot, 1)],
                   [("QT", slot, 1)])
            else:
                cp("act", KT[slot][:, cs], bank(bk), [("ps", bk)], [("KT", slot)])

        def build_th(h):
            ths = h % 2
            stt(T8, dtab, -8.0 * alibi_slope(h), ltab, ALU.mult, ALU.add, ["dtab", "ltab"], ["T8"])
            cp("act", Th[ths][0], T8, ["T8"], [("Thi", ths), ("Th", ths)])
            tt_("dve", Th[ths][1], T8, Th[ths][0], ALU.subtract, ["T8", ("Thi", ths)], [("Th", ths)])

        for j in range(2):
            for tc in range(4):
                proj_piece(0, j, tc)
        build_th(0)
        for hp in range(H // 2):
            slot = hp % 2
            pieces = []
            if hp + 1 < H // 2:
                pieces = [(lambda j=j, tc=tc: proj_piece(hp + 1, j, tc)) for j in range(2) for tc in range(4)]
            for hh in range(2):
                h = 2 * hp + hh
                ths = h % 2
                hooks = []
                if h + 1 < H:
                    hooks.append(lambda h=h: build_th(h + 1))
                if hh == 1:
                    hooks += pieces
                    if hp + 2 < H // 2:
                        hooks.append(lambda hp=hp: load_wqk(hp + 2))
                pb = 64 * hh
                attention(h, QT[slot][hh], KT[slot], ("QT", slot, hh), ("KT", slot),
                          0.125, True, Th[ths], ("Th", ths), V_aug, O_tm, Sb, PT, rden, hooks=hooks)
        P.barrier()
        A.release(m_a1)
        if stop_after == "A2":
            if debug_out:
                dma("sp", [(dbg_d.rearrange("(t p) c -> p t c", p=128), O_tm)], [], [], "dbg")
            return
        mixer_epilogue(s, wv, 0, O_tm, x_d, xr[0])
        P.barrier()
        A.release(m_a)

    def phase_c(s):
        m_c = A.mark()
        V_aug = A.alloc([NT, H, 65], BF16)
        O_tm = A.alloc([NT, DM], BF16)
        m_c2 = A.mark()
        ckvT = A.alloc([2, SEQ], BF16)
        cqT = A.alloc([3, SEQ], BF16)
        QT = [A.alloc([SEQ], BF16) for _ in range(2)]
        KT = [A.alloc([SEQ], BF16) for _ in range(2)]
        ropec = A.alloc([SEQ], F32)
        ropes = A.alloc([SEQ], F32)
        t1 = [A.alloc([512], F32) for _ in range(2)]
        t2 = [A.alloc([512], F32)]
        wuk = A.alloc([2, DM], BF16)
        wuq_c = A.alloc([3, H, 128], BF16)
        Sb = None
        PT = [A.alloc([512], BF16) for _ in range(4)]
        rden = A.alloc([NT], F32)
        m_c1 = A.mark()
        wuv = A.alloc([2, DM], BF16)
        ckvn = [A.alloc([256], BF16) for _ in range(2)]
        cqn = [A.alloc([384], BF16) for _ in range(2)]
        junk = A.alloc([384], F32)
        ss = [A.alloc([2], F32) for _ in range(2)]
        sd2 = [A.alloc([2], F32) for _ in range(2)]
        rs2 = [A.alloc([2], F32) for _ in range(2)]
        load_w(wuv, wuv_d.rearrange("(k p) c -> p k c", p=128), "wuv")
        load_w(wuk, wuk_d.rearrange("(k p) c -> p k c", p=128), "wuk")
        wuq_src = wuq_d.rearrange("(k p) (h c) -> p k h c", p=128, c=96)
        dma("pool", [(wuq_c[:, k, :, 0:96], wuq_src[:, k, :, :]) for k in range(3)], [], ["wuq"], "wuq")
        dma("sp", [(ropec[64:96, :], ropec_d)], [], ["ropec"], "ropec")
        dma("sp", [(ropes[64:96, :], ropes_d), (ropes[96:128, :], ropes_d)], [], ["ropes"], "ropes")
        memset("pool", V_aug[:, :, :, 64:65], 1.0, [], ["V_ones"])
        if c1_parts < 1:
            P.barrier()
            return
        for k in range(3):
            act(wuq_c[:, k, :, 96:112], wuq_c[:, k, :, 80:96], AF.Identity, ["wuq"], [("wuq_c1", k)], scale=-1.0)
            act(wuq_c[:, k, :, 112:128], wuq_c[:, k, :, 64:80], AF.Identity, ["wuq", ("wuq_c1", k)],
                [("wuq_rot", k)])
        rotkeys = [("wuq_rot", k) for k in range(3)]
        def c1_stage_a(tt):
            slot = tt % 2
            bA, bB = (0, 1) if slot == 0 else (2, 3)
            for (bk, w_, ncol, wk_) in ((bA, wdkv, 256, "wdkv"), (bB, wdq, 384, "wdq")):
                for k in range(8):
                    mm(bank(bk, ncol), xTb[:, k, tt * 128:(tt + 1) * 128], w_[:, k, :], k == 0, k == 7,
                       [("xTb", tt), wk_], [("ps", bk)], signal=(k == 7))
            act(junk[:, 0:256], bank(bA, 256), AF.Square, [("ps", bA)], ["junk", ("ss0", slot)],
                accum_out=ss[slot][:, 0:1])
            act(junk[:, 0:384], bank(bB, 384), AF.Square, [("ps", bB), "junk"], ["junk", ("ss1", slot)],
                accum_out=ss[slot][:, 1:2])
            act(sd2[slot][:, 0:1], ss[slot][:, 0:1], AF.Sqrt, [("ss0", slot), "epsc"], [("sd0", slot)],
                bias=epsrms, scale=1.0 / 256.0)
            act(sd2[slot][:, 1:2], ss[slot][:, 1:2], AF.Sqrt, [("ss1", slot), "epsc"], [("sd1", slot)],
                bias=epsrms, scale=1.0 / 384.0)
            recip(rs2[slot], sd2[slot], [("sd0", slot), ("sd1", slot)], [("rs2", slot)])
            stt(ckvn[slot], bank(bA, 256), rs2[slot][:, 0:1], gkv_bc, ALU.mult, ALU.mult,
                [("ps", bA), ("rs2", slot), "gkv"], [("ckvn", slot)])
            stt(cqn[slot], bank(bB, 384), rs2[slot][:, 1:2], gq_bc, ALU.mult, ALU.mult,
                [("ps", bB), ("rs2", slot), "gq"], [("cqn", slot)])

        def c1_stage_b(tt):
            slot = tt % 2
            tb = 4 + slot
            pb_ = bank_bf(tb).rearrange("p (c n) -> p c n", n=128)
            for c in range(2):
                tr(pb_[:, c, :], ckvn[slot][:, c * 128:(c + 1) * 128], [("ckvn", slot)], [("ps", tb)], signal=False)
            for c in range(3):
                tr(pb_[:, 2 + c, :], cqn[slot][:, c * 128:(c + 1) * 128], [("cqn", slot)], [("ps", tb)],
                   signal=(c == 2))
            ceng = "dve" if slot == 0 else "act"
            cp(ceng, ckvT[:, :, tt * 128:(tt + 1) * 128], pb_[:, 0:2, :], [("ps", tb)], [("ckvT", tt)])
            cp(ceng, cqT[:, :, tt * 128:(tt + 1) * 128], pb_[:, 2:5, :], [("ps", tb)], [("cqT", tt)])
            for half in range(2):
                bk = 6 + half
                for c in range(2):
                    mm(bank(bk), ckvT[:, c, tt * 128:(tt + 1) * 128], wuv[:, c, half * 512:(half + 1) * 512],
                       c == 0, c == 1, [("ckvT", tt), "wuv"], [("ps", bk)], signal=(c == 1))
                v_evac(bk, V_aug, tt, half)

        if c1_parts < 2:
            P.barrier()
            return
        c1_stage_a(0)
        for tt in range(NT):
            if tt + 1 < NT:
                c1_stage_a(tt + 1)
            c1_stage_b(tt)
        for tc in range(4 if c1_parts >= 5 else 0):
            sl = tc % 2
            bA, bB = (0, 1) if sl == 0 else (2, 3)
            cs = slice(tc * 512, (tc + 1) * 512)
            for (bk, w_, wk_) in ((bA, wkr, "wkr"), (bB, wkr_rot, "wkr_rot")):
                for k in range(8):
                    mm(ps[64:96, bk * 512:(bk + 1) * 512], w_[:, k, :], xTb[:, k, cs], k == 0, k == 7,
                       [wk_] + xkeys(tc * 512, (tc + 1) * 512), [("ps", bk)], signal=(k == 7))
            tt_("dve", t1[sl][64:96, :], ps[64:96, bA * 512:(bA + 1) * 512], ropec[64:96, cs], ALU.mult,
                [("ps", bA), "ropec"], [("t1", sl)])
            tt_("dve", t2[0][64:96, :], ps[64:96, bB * 512:(bB + 1) * 512], ropes[64:96, cs], ALU.mult,
                [("ps", bB), "ropes"], [("t2", 0)])
            tt_("pool", KT[0][64:96, cs], t1[sl][64:96, :], t2[0][64:96, :], ALU.add,
                [("t1", sl), ("t2", 0)], [("KTr", 0, tc)])
            tt_("pool", KT[1][64:96, cs], t1[sl][64:96, :], t2[0][64:96, :], ALU.add,
                [("t1", sl), ("t2", 0)], [("KTr", 1, tc)])
        P.barrier()
        A.release(m_c1)
        if stop_after == "C1":
            if debug_out:
                dma("sp", [(dbg_d.rearrange("(t p) c -> p t c", p=128)[:, :, 0:64], V_aug[:, :, 0, 0:64]),
                           (dbg_d.rearrange("(t p) c -> p t c", p=128)[0:96, :, 128:256], KT[0][0:96, :].rearrange("p (t c) -> p t c", c=128))], [], [], "dbg")
            return
        wo_off = A.mark()
        wo_c = A.alloc([8, DM], BF16)
        load_w(wo_c, bwo_d.rearrange("(k p) c -> p k c", p=128), "wo", parts=2)
        scale1 = 96.0 ** -0.5
        allckv = [("ckvT", t) for t in range(NT)]
        allcq = [("cqT", t) for t in range(NT)]

        def proj_head_piece(h, tc, part):
            slot = h % 2
            cs = slice(tc * 512, (tc + 1) * 512)
            bA, bB = 7, 7
            sl = tc % 2
            if part == 0:
                c_lo, r_lo = (h * 64, 0) if h + 1 < H else ((h - 1) * 64, 64)
                for c in range(2):
                    mm(bank(bB), wuk[:, c, c_lo:c_lo + 128], ckvT[:, c, cs],
                       c == 0, c == 1, ["wuk"] + allckv[tc * 4:(tc + 1) * 4], [("ps", bB)], signal=(c == 1))
                cp("dve", KT[slot][0:64, cs], ps[r_lo:r_lo + 64, bB * 512:(bB + 1) * 512], [("ps", bB)],
                   [("KT", slot)])
            else:
                for c in range(3):
                    mm(bank(bA), wuq_c[:, c, h, :], cqT[:, c, cs],
                       c == 0, c == 2, rotkeys + allcq[tc * 4:(tc + 1) * 4], [("ps", bA)], signal=(c == 2))
                cp("dve", QT[slot][0:64, cs], ps[0:64, bA * 512:(bA + 1) * 512], [("ps", bA)], [("QT", slot)])
                tt_("dve", t1[sl][64:96, :], ps[64:96, bA * 512:(bA + 1) * 512], ropec[64:96, cs], ALU.mult,
                    [("ps", bA), "ropec"], [("t1", sl)])
                tt_("dve", t2[0][64:96, :], ps[96:128, bA * 512:(bA + 1) * 512], ropes[96:128, cs], ALU.mult,
                    [("ps", bA), "ropes"], [("t2", 0)])
                tt_("pool", QT[slot][64:96, cs], t1[sl][64:96, :], t2[0][64:96, :], ALU.add,
                    [("t1", sl), ("t2", 0)], [("QT", slot)])

        for tc in range(4):
            for part in range(2):
                proj_head_piece(0, tc, part)
        for h in range(H):
            slot = h % 2
            hooks = []
            if h + 1 < H:
                hooks = [(lambda h=h, tc=tc, part=part: proj_head_piece(h + 1, tc, part))
                         for tc in range(4) for part in range(2)]
            attention(h, QT[slot][0:96, :], KT[slot][0:96, :], ("QT", slot), ("KT", slot),
                      scale1, False, None, None, V_aug, O_tm, Sb, PT, rden, hooks=hooks)
        P.barrier()
        A.release(m_c2)
        if stop_after == "C2":
            if debug_out:
                dma("sp", [(dbg_d.rearrange("(t p) c -> p t c", p=128), O_tm)], [], [], "dbg")
            return
        mixer_epilogue(s, wo_c, 1, O_tm, xr[1], xr[2])
        assert A.top <= wo_off, "epilogue tiles overlap the prefetched w_o"
        P.barrier()
        A.release(m_c)

    order = ["P0", "A2", "A", "B", "C1", "C2", "C", "D"]
    lim = order.index(stop_after) if stop_after else len(order)
    for s in range(NSEQ if stop_after is None else 1):
        if s > 0:
            P.barrier()
            P.new_engine_sems()
        phase_p0(s)
        if lim == 0:
            break
        if skip_ab:
            phase_c(s)
            break
        phase_a(s)
        if lim <= 2:
            if debug_out and stop_after == "A":
                dma("sp", [(dbg_d.rearrange("(k p) t -> p k t", p=128), xTb)], [], [], "dbg")
            break
        ffn(s, 0, xr[0], xr[1], False)
        if lim == 3:
            if debug_out:
                dma("sp", [(dbg_d.rearrange("(k p) t -> p k t", p=128), xTb)], [], [], "dbg")
            break
        phase_c(s)
        if lim <= 6:
            if debug_out and stop_after == "C":
                pass
            break
        ffn(s, 1, xr[2], y_d, True)

    P.barrier()
    P.emit(block)
    blockc.__exit__(None, None, None)
    psc.__exit__(None, None, None)
    sb.__exit__(None, None, None)
    return nc


def _constants():
    c = {}
    c["c_ident"] = np.eye(128, dtype=np.float32).astype(ml_dtypes.bfloat16)
    ki = np.arange(128)[:, None]
    cc = np.arange(SEQ)[None, :]
    dist = cc - ki
    c["c_dtab"] = dist.astype(np.float32)
    mult = ((dist >= 0) & (dist <= 128)).astype(np.int64)
    mult = mult + ((dist >= 0) & (dist % 4 == 0) & (dist <= 512))
    mult = mult + ((dist >= 0) & (dist % 16 == 0) & (dist <= 2048))
    lt = np.where(mult > 0, 8.0 * np.log(np.maximum(mult, 1).astype(np.float64)), NEG)
    c["c_ltab"] = lt.astype(np.float32)
    j = np.arange(128)[None, :]
    c["c_cmaskb"] = np.where(j >= ki, 0.0, -30000.0).astype(np.float32).astype(ml_dtypes.bfloat16)
    inv_freq = (np.float32(10000.0) ** (-(np.arange(0, 32, 2, dtype=np.float32)) / np.float32(32.0))).astype(np.float32)
    ang = (np.arange(SEQ, dtype=np.float32)[:, None] * inv_freq[None, :]).astype(np.float32)
    cos = np.cos(ang).astype(np.float32).T
    sin = np.sin(ang).astype(np.float32).T
    c["c_ropec"] = np.ascontiguousarray(np.concatenate([cos, cos], axis=0))
    c["c_ropes"] = np.ascontiguousarray(np.concatenate([sin, sin], axis=0))
    return c


def make_in_maps(inputs):
    f = lambda a: np.ascontiguousarray(np.asarray(a, dtype=np.float32))
    x = f(inputs["x"]).reshape(NCORES, NSEQ * SEQ, DM)
    shared = {
        "a_w_qkv": f(inputs["a_w_qkv"])[0],
        "a_w_o": f(inputs["a_w_o"])[0],
        "kv_w_dkv": f(inputs["kv_w_dkv"]),
        "kv_norm_g": f(inputs["kv_norm_g"]).reshape(1, 256),
        "kv_w_kr": f(inputs["kv_w_kr"]),
        "kv_w_uk": f(inputs["kv_w_uk"]),
        "kv_w_uv": f(inputs["kv_w_uv"]),
        "b_w_dq": f(inputs["b_w_dq"])[0],
        "b_q_norm_g": f(inputs["b_q_norm_g"]).reshape(1, 384),
        "b_w_uq": f(inputs["b_w_uq"])[0],
        "b_w_o": f(inputs["b_w_o"])[0],
        "ffn_w_in": f(inputs["ffn_w_in"]),
        "ffn_w_out": f(inputs["ffn_w_out"]),
        "ln_mix_g": f(inputs["ln_mix_g"]),
        "ln_mix_b": f(inputs["ln_mix_b"]),
        "ln_ffn_g": f(inputs["ln_ffn_g"]),
        "ln_ffn_b": f(inputs["ln_ffn_b"]),
    }
    cwv = f(inputs["ffn_conv_w"]).reshape(2, 3, 2 * NG, 128)
    shared["ffn_conv_w"] = np.ascontiguousarray(cwv.transpose(3, 0, 2, 1)).reshape(128, -1)
    cbv = f(inputs["ffn_conv_b"]).reshape(2, 2 * NG, 128)
    shared["ffn_conv_b"] = np.ascontiguousarray(cbv.transpose(2, 0, 1)).reshape(128, -1)
    shared.update(_constants())
    return [dict(shared, x=np.ascontiguousarray(x[i])) for i in range(NCORES)]


_NC_CACHE = {}


def kernel(**inputs):
    in_maps = make_in_maps(inputs)
    if "nc" not in _NC_CACHE:
        _NC_CACHE["nc"] = build_program()
    nc = _NC_CACHE["nc"]
    res = run_bass_kernel_spmd(nc, in_maps, core_ids=list(range(NCORES)))
    out = np.stack([np.asarray(r["y"], dtype=np.float32) for r in res.results], axis=0)
    return out.reshape(NCORES * NSEQ, SEQ, DM)
```

```python
import math
import numpy as np
import ml_dtypes
import concourse.bass as bass
import concourse.mybir as mybir
from concourse.bass_utils import run_bass_kernel_spmd

F32 = mybir.dt.float32
BF16 = mybir.dt.bfloat16
AF = mybir.ActivationFunctionType
ALU = mybir.AluOpType

NCORES = 8
SEQ = 2048
DM = 1024
NSEQ = 2
NT = SEQ // 128
H = 16
DFF = 2816
NG = DFF // 128
ALPHA = 4.0 ** 0.25
LN_EPS = 1e-5
RMS_EPS = 1e-6
NEG = -1.0e9


class Prog:
    ENG = ("pe", "act", "dve", "pool", "sp")

    def __init__(self, nc):
        self.nc = nc
        self.q = {e: [] for e in self.ENG}
        self.esem = {}
        self.ecnt = {}
        self.waited = {e: {} for e in self.ENG}
        self.res = {}
        self.dsem = {}
        self.pending = {e: [] for e in self.ENG}
        self.last = {e: None for e in self.ENG}
        self.gen = 0
        self.new_engine_sems()

    def new_engine_sems(self):
        for e in self.ENG:
            if e == "sp":
                continue
            self.esem[e] = self.nc.alloc_semaphore(name=f"es_{e}_{self.gen}")
            self.ecnt[e] = 0
        self.gen += 1

    def _waits(self, eng, deps):
        waits = {}
        for tok in deps:
            sem, val, teng = tok
            if eng == "pe" and teng == "pe":
                continue
            assert val is not None, "dependency on unresolved (non-signalled) op"
            k = id(sem)
            if self.waited[eng].get(k, 0) >= val:
                continue
            if k not in waits or waits[k][1] < val:
                waits[k] = (sem, val)
        for k, (s, v) in waits.items():
            self.waited[eng][k] = v
        return list(waits.values())

    def op(self, eng, fn, reads=(), writes=(), signal=True, dma=None):
        deps = []
        for r in reads:
            st = self.res.get(r)
            if st is not None and st[0] is not None:
                deps.append(st[0])
        for w in writes:
            st = self.res.get(w)
            if st is not None:
                if st[0] is not None:
                    deps.append(st[0])
                deps.extend(st[1])
        waits = self._waits(eng, deps)
        if dma is not None:
            key, n = dma
            ent = self.dsem.get(key)
            if ent is None:
                ent = [self.nc.alloc_semaphore(name=f"ds_{len(self.dsem)}"), 0]
                self.dsem[key] = ent
            ent[1] += 16 * n
            tok = [ent[0], ent[1], "dma"]
            inc = (ent[0], 16)
        elif signal:
            self.ecnt[eng] += 1
            tok = [self.esem[eng], self.ecnt[eng], eng]
            inc = (self.esem[eng], 1)
            for p in self.pending[eng]:
                p[0], p[1], p[2] = tok[0], tok[1], tok[2]
            self.pending[eng] = []
            self.last[eng] = tok
        else:
            tok = [None, None, eng]
            inc = None
            self.pending[eng].append(tok)
        self.q[eng].append((waits, fn, inc))
        for r in reads:
            self.res.setdefault(r, [None, []])[1].append(tok)
        for w in writes:
            self.res[w] = [tok, []]
        return tok

    def barrier(self):
        for e in self.ENG:
            assert not self.pending[e], f"pending unsignalled ops on {e} at barrier"
        toks = [t for t in self.last.values() if t is not None]
        toks += [[ent[0], ent[1], "dma"] for ent in self.dsem.values() if ent[1] > 0]
        for e in self.ENG:
            w = self._waits(e, toks)
            if w:
                self.q[e].append((w, None, None))
        self.res = {}

    def emit(self, block):
        m = {"pe": block.tensor, "act": block.scalar, "dve": block.vector,
             "pool": block.gpsimd, "sp": block.sync}
        for e in self.ENG:
            def body(eng, e=e):
                for waits, fn, inc in self.q[e]:
                    for (s, v) in waits:
                        eng.wait_ge(s, v)
                    if fn is None:
                        continue
                    ins = fn(eng)
                    if inc is not None:
                        if isinstance(ins, (list, tuple)):
                            for i in ins:
                                i.then_inc(inc[0], inc[1])
                        else:
                            ins.then_inc(inc[0], inc[1])
            m[e](body)


class Arena:
    def __init__(self, base, words):
        self.base = base
        self.words = words
        self.top = 0

    def mark(self):
        return self.top

    def release(self, m):
        self.top = m

    def alloc(self, shape, dtype):
        n = 1
        for s in shape:
            n *= s
        nbytes = n * (2 if dtype == BF16 else 4)
        w = (nbytes + 3) // 4
        w = (w + 7) // 8 * 8
        assert self.top + w <= self.words, f"SBUF arena overflow: need {self.top + w} have {self.words}"
        ap = self.base[:, self.top:self.top + w]
        self.top += w
        if dtype == BF16:
            ap = ap.bitcast(BF16)
        ap = ap[:, 0:n]
        if len(shape) == 1:
            return ap
        names = " ".join(f"d{i}" for i in range(len(shape)))
        kw = {f"d{i}": shape[i] for i in range(1, len(shape))}
        return ap.rearrange(f"p ({names}) -> p {names}", **kw)


def alibi_slope(h):
    return 2.0 ** (-8.0 * (h + 1) / H)


def build_program(stop_after=None, debug_out=False, skip_ab=False, c1_parts=5):
    nc = bass.Bass("TRN2", target_bir_lowering=False)
    dt = lambda name, shape, dtype=F32, kind="ExternalInput": nc.dram_tensor(name, shape, dtype, kind=kind)
    x_d = dt("x", [NSEQ * SEQ, DM]).ap()
    wqkv_d = dt("a_w_qkv", [DM, 3 * DM]).ap()
    awo_d = dt("a_w_o", [DM, DM]).ap()
    wdkv_d = dt("kv_w_dkv", [DM, 256]).ap()
    gkv_t = dt("kv_norm_g", [1, 256])
    wkr_d = dt("kv_w_kr", [DM, 32]).ap()
    wuk_d = dt("kv_w_uk", [256, DM]).ap()
    wuv_d = dt("kv_w_uv", [256, DM]).ap()
    wdq_d = dt("b_w_dq", [DM, 384]).ap()
    gq_t = dt("b_q_norm_g", [1, 384])
    wuq_d = dt("b_w_uq", [384, 1536]).ap()
    bwo_d = dt("b_w_o", [DM, DM]).ap()
    win_d = dt("ffn_w_in", [2, DM, 2 * DFF]).ap()
    cw_d = dt("ffn_conv_w", [128, 2 * 2 * NG * 3]).ap()
    cb_d = dt("ffn_conv_b", [128, 2 * 2 * NG]).ap()
    wout_d = dt("ffn_w_out", [2, DFF, DM]).ap()
    lng_t = {}
    for nm in ("ln_mix_g", "ln_mix_b", "ln_ffn_g", "ln_ffn_b"):
        lng_t[nm] = dt(nm, [2, DM])
    ident_d = dt("c_ident", [128, 128], BF16).ap()
    dtab_d = dt("c_dtab", [128, SEQ]).ap()
    ltab_d = dt("c_ltab", [128, SEQ]).ap()
    cmask_d = dt("c_cmask", [128, 128]).ap()
    cmaskb_d = dt("c_cmaskb", [128, 128], BF16).ap()
    ropec_d = dt("c_ropec", [32, SEQ]).ap()
    ropes_d = dt("c_ropes", [32, SEQ]).ap()
    y_d = dt("y", [NSEQ * SEQ, DM], F32, "ExternalOutput").ap()
    xr = [dt(f"xr{i}", [NSEQ * SEQ, DM], F32, "Internal").ap() for i in range(3)]
    dbg_d = dt("dbg", [SEQ, DM], BF16, "ExternalOutput").ap() if debug_out else None

    def bcast_rows(t, row, n):
        return bass.AP(t, row * n, [(0, 128), (1, n)])

    WORDS = 53200
    sb = nc.sbuf_tensor("arena", [128, WORDS], F32)
    sbuf = sb.__enter__()
    psc = nc.psum_tensor("psum", [128, 4096], F32)
    ps = psc.__enter__()
    blockc = nc.Block()
    block = blockc.__enter__()

    P = Prog(nc)
    A = Arena(sbuf, WORDS)

    def bank(b, n=512):
        return ps[:, b * 512:b * 512 + n]

    def bank_bf(b):
        return ps[:, b * 512:(b + 1) * 512].bitcast(BF16)

    ident = A.alloc([128], BF16)
    cw = A.alloc([2, 2 * NG, 3], F32)
    cb = A.alloc([2, 2 * NG], F32)
    gkv_bc = A.alloc([256], F32)
    gq_bc = A.alloc([384], F32)
    cmask = A.alloc([128], F32)
    cmaskb = A.alloc([128], BF16)
    halo = A.alloc([2 * NG, 2], F32)
    xTb = A.alloc([8, SEQ], BF16)

    def ld(dst, src, key, eng="sp", n=1):
        P.op(eng, lambda e: e.dma_start(out=dst, in_=src), writes=[key], dma=(key, 1))

    ld(ident, ident_d, "ident")
    ld(cw.rearrange("p l c j -> p (l c j)"), cw_d, "cw")
    ld(cb.rearrange("p l c -> p (l c)"), cb_d, "cb")
    ld(gkv_bc, bcast_rows(gkv_t, 0, 256), "gkv")
    ld(gq_bc, bcast_rows(gq_t, 0, 384), "gq")
    ld(cmask, cmask_d, "cmask")
    ld(cmaskb, cmaskb_d, "cmaskb")

    def xkeys(t0, t1):
        return [("xTb", t) for t in range(t0 // 128, (t1 + 127) // 128)]

    def mm(out, lhsT, rhs, start, stop, reads, writes, signal=True, **kw):
        P.op("pe", lambda e: e.matmul(out, lhsT=lhsT, rhs=rhs, start=start, stop=stop, **kw),
             reads=reads, writes=writes, signal=signal)

    def tr(out, in_, reads, writes, signal=True):
        P.op("pe", lambda e: e.transpose(out=out, in_=in_, identity=ident), reads=list(reads) + ["ident"],
             writes=writes, signal=signal)

    def act(out, in_, func, reads, writes, **kw):
        P.op("act", lambda e: e.activation(out=out, in_=in_, func=func, **kw), reads=reads, writes=writes)

    def stt(out, in0, scalar, in1, op0, op1, reads, writes):
        P.op("dve", lambda e: e.scalar_tensor_tensor(out=out, in0=in0, scalar=scalar, in1=in1, op0=op0, op1=op1),
             reads=reads, writes=writes)

    def tt_(eng, out, in0, in1, op, reads, writes):
        P.op(eng, lambda e: e.tensor_tensor(out=out, in0=in0, in1=in1, op=op), reads=reads, writes=writes)

    def ts_(eng, out, in0, s1, s2, op0, op1, reads, writes):
        if op1 is None:
            P.op(eng, lambda e: e.tensor_scalar(out=out, in0=in0, scalar1=s1, scalar2=None, op0=op0),
                 reads=reads, writes=writes)
        else:
            P.op(eng, lambda e: e.tensor_scalar(out=out, in0=in0, scalar1=s1, scalar2=s2, op0=op0, op1=op1),
                 reads=reads, writes=writes)

    def cp(eng, out, in_, reads, writes):
        if eng == "act":
            act(out, in_, AF.Copy, reads, writes)
        else:
            P.op(eng, lambda e: e.tensor_copy(out=out, in_=in_), reads=reads, writes=writes)

    def recip(out, in_, reads, writes):
        P.op("dve", lambda e: e.reciprocal(out=out, in_=in_), reads=reads, writes=writes)

    def memset(eng, ap, val, reads, writes):
        P.op(eng, lambda e: e.memset(ap, val), reads=reads, writes=writes)

    def dma(eng, pairs, reads, writes, key):
        P.op(eng, lambda e: [e.dma_start(out=o, in_=i) for (o, i) in pairs], reads=reads, writes=writes,
             dma=(key, len(pairs)))

    def load_w(dst, src, key, parts=1):
        k = dst.shape[1]
        step = (k + parts - 1) // parts
        pairs = [(dst[:, a:min(k, a + step)], src[:, a:min(k, a + step)]) for a in range(0, k, step)]
        dma("pool", pairs, [], [key], key)

    def transpose_tile(src, nch, src_key, dsts, bank_id):
        pb = bank_bf(bank_id).rearrange("p (c n) -> p c n", n=128)
        for c in range(nch):
            tr(pb[:, c, :], src[:, c * 128:(c + 1) * 128], [src_key], [("ps", bank_id)], signal=(c == nch - 1))
        for (eng, dst, c0, n, wkeys) in dsts:
            cp(eng, dst, pb[:, c0:c0 + n, :], [("ps", bank_id)], wkeys)

    epsc = A.alloc([4], F32)
    epsln = epsc[:, 0:1]
    epsrms = epsc[:, 1:2]
    memset("pool", epsc[:, 0:1], LN_EPS, [], ["epsc0"])
    memset("pool", epsc[:, 1:2], RMS_EPS, ["epsc0"], ["epsc"])

    def ln_tiles_alloc():
        d = {}
        d["xres"] = [A.alloc([DM], F32) for _ in range(2)]
        d["z"] = [A.alloc([DM], F32) for _ in range(2)]
        d["xo"] = [A.alloc([DM], F32) for _ in range(2)]
        d["xb"] = [A.alloc([DM], BF16) for _ in range(2)]
        d["st"] = [A.alloc([2, 6], F32) for _ in range(2)]
        d["mv"] = [A.alloc([2], F32) for _ in range(2)]
        d["sd"] = [A.alloc([1], F32) for _ in range(2)]
        d["rstd"] = [A.alloc([1], F32) for _ in range(2)]
        d["g"] = A.alloc([DM], F32)
        d["b"] = A.alloc([DM], F32)
        return d

    def ln_load_params(L, gname, bname, layer):
        ld(L["g"], bcast_rows(lng_t[gname], layer, DM), "ln_g")
        ld(L["b"], bcast_rows(lng_t[bname], layer, DM), "ln_b")

    def load_xres(L, src_d, row0, slot):
        dma("sp", [(L["xres"][slot], src_d[row0:row0 + 128, :])], [], [("xres", slot)], ("xres", slot))

    def ln_stage1(L, slot, ybank):
        yps = ps[:, ybank * 512:ybank * 512 + 1024]
        xres, z = L["xres"][slot], L["z"][slot]
        st, mv, sd = L["st"][slot], L["mv"][slot], L["sd"][slot]
        stt(z, xres, ALPHA, yps, ALU.mult, ALU.add,
            [("xres", slot), ("ps", ybank), ("ps", ybank + 1)], [("z", slot)])
        P.op("dve", lambda e: e.bn_stats(out=st[:, 0, :], in_=z[:, 0:512]), reads=[("z", slot)],
             writes=[("st0", slot)])
        P.op("dve", lambda e: e.bn_stats(out=st[:, 1, :], in_=z[:, 512:1024]), reads=[("z", slot)],
             writes=[("st1", slot)])
        P.op("dve", lambda e: e.bn_aggr(out=mv, in_=st.rearrange("p a b -> p (a b)")),
             reads=[("st0", slot), ("st1", slot)], writes=[("mv", slot)])
        act(sd, mv[:, 1:2], AF.Sqrt, [("mv", slot), "epsc"], [("sd", slot)], bias=epsln, scale=1.0)

    def ln_stage2(L, slot, dst_d, row0, tt, do_transpose, tbank):
        z, xo, xb = L["z"][slot], L["xo"][slot], L["xb"][slot]
        mv, sd, rstd = L["mv"][slot], L["sd"][slot], L["rstd"][slot]
        recip(rstd, sd, [("sd", slot)], [("rstd", slot)])
        stt(z, z, mv[:, 0:1], L["g"], ALU.subtract, ALU.mult,
            [("z", slot), ("mv", slot), "ln_g"], [("z", slot)])
        stt(xo, z, rstd, L["b"], ALU.mult, ALU.add,
            [("z", slot), ("rstd", slot), "ln_b"], [("xo", slot)])
        dma("pool", [(dst_d[row0:row0 + 128, :], xo)], [("xo", slot)], [], ("xo_st", slot))
        if do_transpose:
            cp("act", xb, xo, [("xo", slot)], [("xb", slot)])
            dst = xTb[:, :, tt * 128:(tt + 1) * 128]
            transpose_tile(xb, 8, ("xb", slot), [("act", dst, 0, 8, [("xTb", tt)])], tbank)

    def oacc(qt):
        b = 2 + qt // 7
        c = (qt % 7) * 65
        return ps[:, b * 512 + c: b * 512 + c + 65]

    SBANKS0 = (0, 1, 5, 6)
    SBANKS1 = (0, 1, 5)
    SKIP_T = 50.0

    def attention(h, QT, KT, qkey, kkey, scale, layer0, Th, thkey, V_aug, O_tm, Sb, PT, rden, hooks=()):
        chunks = []
        slope = alibi_slope(h)
        for kb in range(NT):
            for cc in range(kb // 4, 4):
                q0 = max(128 * kb, 512 * cc)
                N = 512 * (cc + 1) - q0
                if layer0:
                    while N > 0 and slope * (q0 + N - 128 - 128 * kb - 127) > SKIP_T:
                        N -= 128
                    if N <= 0:
                        continue
                chunks.append((kb, q0, N))
        n = len(chunks)
        started = set()
        SBANKS = SBANKS0 if layer0 else SBANKS1
        LA = len(SBANKS) - 1
        NS = len(SBANKS)

        def qk(i):
            kb, q0, N = chunks[i]
            sbk = SBANKS[i % NS]
            if (not layer0) and q0 == 128 * kb:
                mm(bank(sbk, N), KT[:, kb * 128:(kb + 1) * 128], QT[:, q0:q0 + N], True, False,
                   [qkey, kkey], [("ps", sbk)], signal=False)
                mm(bank(sbk, 128), ident, cmaskb, False, True, ["ident", "cmaskb"], [("ps", sbk)])
            elif layer0:
                c0 = q0 - 128 * kb
                mm(bank(sbk, N), KT[:, kb * 128:(kb + 1) * 128], QT[:, q0:q0 + N], True, False,
                   [qkey, kkey], [("ps", sbk)], signal=False)
                mm(bank(sbk, N), ident, Th[0][:, c0:c0 + N], False, False, ["ident", thkey], [("ps", sbk)],
                   signal=False)
                mm(bank(sbk, N), ident, Th[1][:, c0:c0 + N], False, True, ["ident", thkey], [("ps", sbk)])
            else:
                mm(bank(sbk, N), KT[:, kb * 128:(kb + 1) * 128], QT[:, q0:q0 + N], True, True,
                   [qkey, kkey], [("ps", sbk)])

        def mid(i):
            kb, q0, N = chunks[i]
            sbk = SBANKS[i % NS]
            s3 = i % NS
            S = bank(sbk, N)
            if layer0:
                act(PT[s3][:, 0:N], S, AF.Exp, [("ps", sbk)], [("PT", s3)], scale=scale)
            else:
                act(PT[s3][:, 0:N], S, AF.Exp, [("ps", sbk)], [("PT", s3)], scale=scale)

        def pv(i):
            kb, q0, N = chunks[i]
            s3 = i % NS
            nq = N // 128
            for j in range(nq):
                qt = q0 // 128 + j
                ob = 2 + qt // 7
                first = ob not in started
                started.add(ob)
                mm(oacc(qt), PT[s3][:, j * 128:(j + 1) * 128], V_aug[:, kb, h, :],
                   first, (kb == qt),
                   [("PT", s3), ("V_aug", kb, 0), ("V_aug", kb, 1), "V_ones"], ["oacc"],
                   signal=(j == nq - 1), skip_group_check=True)

        for i in range(min(LA, n)):
            qk(i)
            mid(i)
        hooks = list(hooks)
        nh = len(hooks)
        hook_at = {}
        for j, fn in enumerate(hooks):
            hook_at.setdefault(min(n - 1, 1 + (j * max(1, n - 2)) // max(1, nh)), []).append(fn)
        for i in range(n):
            if i + LA < n:
                qk(i + LA)
                mid(i + LA)
            for fn in hook_at.get(i, ()):
                fn()
            pv(i)
        for g, (q_a, q_b) in enumerate(((0, 7), (7, 14), (14, 16))):
            nq = q_b - q_a
            ob = ps[:, (2 + g) * 512:(2 + g) * 512 + nq * 65].rearrange("p (q c) -> p q c", c=65)
            recip(rden[:, q_a:q_b], ob[:, :, 64], ["oacc"], [("rden", g)])
            rb = bass.AP(rden.tensor, rden.offset + q_a, [list(rden.ap[0]), [1, nq], [0, 64]])
            tt_("dve", O_tm[:, q_a:q_b, h * 64:(h + 1) * 64], ob[:, :, 0:64], rb, ALU.mult,
                ["oacc", ("rden", g)], [("O_tm", h, g)])

    def mixer_epilogue(s, wo, layer, O_tm, src_d, dst_d):
        L = ln_tiles_alloc()
        ln_load_params(L, "ln_mix_g", "ln_mix_b", layer)
        for tt in range(NT):
            dst = xTb[:, :, tt * 128:(tt + 1) * 128]
            transpose_tile(O_tm[:, tt, :], 8, "O_tm_all", [("dve" if tt % 2 == 0 else "act", dst, 0, 8,
                                                            [("xTb", tt)])], tt % 2)
        def y_mm(tt):
            yb = 2 + 2 * (tt % 2)
            for half in range(2):
                for k in range(8):
                    mm(bank(yb + half), xTb[:, k, tt * 128:(tt + 1) * 128], wo[:, k, half * 512:(half + 1) * 512],
                       k == 0, k == 7, [("xTb", tt), "wo"], [("ps", yb + half)], signal=(k == 7))

        load_xres(L, src_d, s * SEQ, 0)
        load_xres(L, src_d, s * SEQ + 128, 1)
        y_mm(0)
        for tt in range(NT):
            slot = tt % 2
            if tt + 1 < NT:
                y_mm(tt + 1)
            ln_stage1(L, slot, 2 + 2 * slot)
            if tt + 2 < NT:
                load_xres(L, src_d, s * SEQ + (tt + 2) * 128, slot)
            if tt > 0:
                ln_stage2(L, (tt - 1) % 2, dst_d, s * SEQ + (tt - 1) * 128, tt - 1, True, 6 + (tt - 1) % 2)
        ln_stage2(L, (NT - 1) % 2, dst_d, s * SEQ + (NT - 1) * 128, NT - 1, True, 6 + (NT - 1) % 2)

    def ffn(s, layer, src_d, dst_d, final):
        m0 = A.mark()
        hT = A.alloc([NG, 1024], BF16)
        wout = A.alloc([NG, DM], BF16)
        win = [A.alloc([2, 8, 128], BF16) for _ in range(3)]
        acc = [[A.alloc([1024], F32) for _ in range(2)] for _ in range(2)]
        L = ln_tiles_alloc()
        load_w(wout, wout_d[layer].rearrange("(g p) c -> p g c", p=128), "wout", parts=4)
        ln_load_params(L, "ln_ffn_g", "ln_ffn_b", layer)
        wsrc = win_d[layer]

        def load_win(g):
            slot = g % 3
            pairs = [(win[slot][:, p], wsrc[:, p * DFF + g * 128: p * DFF + (g + 1) * 128]
                      .rearrange("(k q) c -> q k c", q=128)) for p in range(2)]
            dma("pool", pairs, [], [("win", slot)], ("win", slot))

        pu_ctr = 0
        for hf in range(2):
            t0 = hf * 1024
            load_win(0)
            load_win(1)
            for g in range(NG):
                if g + 2 < NG:
                    load_win(g + 2)
                wslot = g % 3
                aslot = g % 2
                for part in range(2):
                    c = part * NG + g
                    pslot = pu_ctr % 4
                    pu_ctr += 1
                    b0 = 2 * pslot
                    pu = ps[:, b0 * 512:b0 * 512 + 1024]
                    a = acc[part][aslot]
                    akey = ("acc", part, aslot)
                    for tc in range(2):
                        for k in range(8):
                            mm(bank(b0 + tc), win[wslot][:, part, k, :],
                               xTb[:, k, t0 + tc * 512:t0 + (tc + 1) * 512], k == 0, k == 7,
                               [("win", wslot)] + xkeys(t0 + tc * 512, t0 + (tc + 1) * 512),
                               [("ps", b0 + tc)], signal=(k == 7))
                    pkeys = [("ps", b0), ("ps", b0 + 1)]
                    w2, w1, w0 = (cw[:, layer, c, j:j + 1] for j in (2, 1, 0))
                    act(a, pu, AF.Identity, pkeys + ["cw", "cb"], [akey], bias=cb[:, layer, c:c + 1], scale=w2)
                    stt(a[:, 1:1024], pu[:, 0:1023], w1, a[:, 1:1024], ALU.mult, ALU.add,
                        pkeys + [akey, "cw"], [akey])
                    stt(a[:, 2:1024], pu[:, 0:1022], w0, a[:, 2:1024], ALU.mult, ALU.add,
                        pkeys + [akey, "cw"], [akey])
                    if hf == 1:
                        stt(a[:, 0:2], halo[:, c, 0:2], w0, a[:, 0:2], ALU.mult, ALU.add,
                            [akey, ("halo", c), "cw"], [akey])
                        stt(a[:, 0:1], halo[:, c, 1:2], w1, a[:, 0:1], ALU.mult, ALU.add,
                            [akey, ("halo", c), "cw"], [akey])
                    else:
                        cp("dve", halo[:, c, :], pu[:, 1022:1024], pkeys, [("halo", c)])
                ag, av = acc[0][aslot], acc[1][aslot]
                act(ag, ag, AF.Silu, [("acc", 0, aslot)], [("acc", 0, aslot)])
                tt_("pool", hT[:, g, :], ag, av, ALU.mult, [("acc", 0, aslot), ("acc", 1, aslot)], [("hT", g)])
            def y_mm2(ti):
                yb = 2 * (ti % 2)
                for half in range(2):
                    for g in range(NG):
                        mm(bank(yb + half), hT[:, g, ti * 128:(ti + 1) * 128],
                           wout[:, g, half * 512:(half + 1) * 512], g == 0, g == NG - 1,
                           [("hT", g), "wout"], [("ps", yb + half)], signal=(g == NG - 1))

            load_xres(L, src_d, s * SEQ + t0, 0)
            load_xres(L, src_d, s * SEQ + t0 + 128, 1)
            y_mm2(0)
            for ti in range(8):
                tt = hf * 8 + ti
                slot = ti % 2
                if ti + 1 < 8:
                    y_mm2(ti + 1)
                ln_stage1(L, slot, 2 * slot)
                if ti + 2 < 8:
                    load_xres(L, src_d, s * SEQ + (tt + 2) * 128, slot)
                if ti > 0:
                    ln_stage2(L, (ti - 1) % 2, dst_d, s * SEQ + (tt - 1) * 128, tt - 1, not final, 4 + (ti - 1) % 2)
            ln_stage2(L, 1, dst_d, s * SEQ + (hf * 8 + 7) * 128, hf * 8 + 7, not final, 5)
        P.barrier()
        A.release(m0)

    def phase_p0(s):
        m = A.mark()
        xt = [A.alloc([DM], F32) for _ in range(2)]
        xb0 = [A.alloc([DM], BF16) for _ in range(2)]
        for tt in range(NT):
            slot = tt % 2
            dma("sp", [(xt[slot], x_d[s * SEQ + tt * 128: s * SEQ + (tt + 1) * 128, :])], [], [("xt", slot)],
                ("xt", slot))
            cp("act", xb0[slot], xt[slot], [("xt", slot)], [("xb0", slot)])
            transpose_tile(xb0[slot], 8, ("xb0", slot),
                           [("dve", xTb[:, :, tt * 128:(tt + 1) * 128], 0, 8, [("xTb", tt)])], tt % 2)
        P.barrier()
        A.release(m)

    def v_evac(bk, V_aug, tt, half):
        src = bank(bk).rearrange("p (h c) -> p h c", c=64)
        dst = V_aug[:, tt, half * 8:(half + 1) * 8, 0:64]
        cp("act" if half == 0 else "dve", dst, src, [("ps", bk), "V_ones"], [("V_aug", tt, half)])

    def phase_a(s):
        m_a = A.mark()
        V_aug = A.alloc([NT, H, 65], BF16)
        O_tm = A.alloc([NT, DM], BF16)
        wv = A.alloc([8, DM], BF16)
        m_a1 = A.mark()
        dtab = A.alloc([SEQ], F32)
        ltab = A.alloc([SEQ], F32)
        T8 = A.alloc([SEQ], F32)
        Th = [[A.alloc([SEQ], BF16) for _ in range(2)] for _ in range(2)]
        QT = [[A.alloc([SEQ], BF16) for _ in range(2)] for _ in range(2)]
        KT = [A.alloc([SEQ], BF16) for _ in range(2)]
        wqk = [A.alloc([2, 8, 128], BF16) for _ in range(2)]
        for sl_ in range(2):
            memset("pool", QT[sl_][0][64:128, :], 0.0, [], [("QTz", sl_, 0)])
            memset("pool", QT[sl_][1][0:64, :], 0.0, [], [("QTz", sl_, 1)])
        Sb = None
        PT = [A.alloc([512], BF16) for _ in range(4)]
        rden = A.alloc([NT], F32)
        load_w(wv, wqkv_d[:, 2 * DM:3 * DM].rearrange("(k p) c -> p k c", p=128), "wo", parts=2)
        ld(dtab, dtab_d, "dtab")
        ld(ltab, ltab_d, "ltab")

        def load_wqk(hp):
            slot = hp % 2
            pairs = [(wqk[slot][:, j], wqkv_d[:, j * DM + hp * 128: j * DM + (hp + 1) * 128]
                      .rearrange("(k q) c -> q k c", q=128)) for j in range(2)]
            dma("pool", pairs, [], [("wqk", slot)], ("wqk", slot))

        load_wqk(0)
        load_wqk(1)
        memset("pool", V_aug[:, :, :, 64:65], 1.0, [], ["V_ones"])
        for tt in range(NT):
            for half in range(2):
                bk = 5 + (2 * tt + half) % 3
                for k in range(8):
                    mm(bank(bk), xTb[:, k, tt * 128:(tt + 1) * 128], wv[:, k, half * 512:(half + 1) * 512],
                       k == 0, k == 7, [("xTb", tt), "wo"], [("ps", bk)], signal=(k == 7))
                v_evac(bk, V_aug, tt, half)
        load_w(wv, awo_d.rearrange("(k p) c -> p k c", p=128), "wo", parts=2)

        pctr = [0]

        def proj_piece(hp, j, tc):
            slot = hp % 2
            bk = 7
            for k in range(8):
                mm(bank(bk), wqk[slot][:, j, k, :], xTb[:, k, tc * 512:(tc + 1) * 512], k == 0, k == 7,
                   [("wqk", slot)] + xkeys(tc * 512, (tc + 1) * 512), [("ps", bk)], signal=(k == 7))
            cs = slice(tc * 512, (tc + 1) * 512)
            if j == 0:
                cp("act", QT[slot][0][0:64, cs], ps[0:64, bk * 512:(bk + 1) * 512], [("ps", bk), ("QTz", slot, 0)],
                   [("QT", slot, 0)])
                cp("act", QT[slot][1][64:128, cs], ps[64:128, bk * 512:(bk + 1) * 512], [("ps", bk), ("QTz", slot, 1)],
                   [("QT", slot, 1)])
            else:
                cp("act", KT[slot][:, cs], bank(bk), [("ps", bk)], [("KT", slot)])

        def build_th(h):
            ths = h % 2
            stt(T8, dtab, -8.0 * alibi_slope(h), ltab, ALU.mult, ALU.add, ["dtab", "ltab"], ["T8"])
            cp("act", Th[ths][0], T8, ["T8"], [("Thi", ths), ("Th", ths)])
            tt_("dve", Th[ths][1], T8, Th[ths][0], ALU.subtract, ["T8", ("Thi", ths)], [("Th", ths)])

        for j in range(2):
            for tc in range(4):
                proj_piece(0, j, tc)
        build_th(0)
        for hp in range(H // 2):
            slot = hp % 2
            pieces = []
            if hp + 1 < H // 2:
                pieces = [(lambda j=j, tc=tc: proj_piece(hp + 1, j, tc)) for j in range(2) for tc in range(4)]
            for hh in range(2):
                h = 2 * hp + hh
                ths = h % 2
                hooks = []
                if h + 1 < H:
                    hooks.append(lambda h=h: build_th(h + 1))
                if hh == 1:
                    hooks += pieces
                    if hp + 2 < H // 2:
                        hooks.append(lambda hp=hp: load_wqk(hp + 2))
                pb = 64 * hh
                attention(h, QT[slot][hh], KT[slot], ("QT", slot, hh), ("KT", slot),
                          0.125, True, Th[ths], ("Th", ths), V_aug, O_tm, Sb, PT, rden, hooks=hooks)
        P.barrier()
        A.release(m_a1)
        if stop_after == "A2":
            if debug_out:
                dma("sp", [(dbg_d.rearrange("(t p) c -> p t c", p=128), O_tm)], [], [], "dbg")
            return
        mixer_epilogue(s, wv, 0, O_tm, x_d, xr[0])
        P.barrier()
        A.release(m_a)

    def phase_c(s):
        m_c = A.mark()
        V_aug = A.alloc([NT, H, 65], BF16)
        O_tm = A.alloc([NT, DM], BF16)
        m_c2 = A.mark()
        ckvT = A.alloc([2, SEQ], BF16)
        cqT = A.alloc([3, SEQ], BF16)
        QT = [A.alloc([SEQ], BF16) for _ in range(2)]
        KT = [A.alloc([SEQ], BF16) for _ in range(2)]
        ropec = A.alloc([SEQ], F32)
        ropes = A.alloc([SEQ], F32)
        t1 = [A.alloc([512], F32) for _ in range(2)]
        t2 = [A.alloc([512], F32) for _ in range(2)]
        wuk = A.alloc([2, DM], BF16)
        wuq_c = A.alloc([3, H, 128], BF16)
        Sb = [A.alloc([512], F32) for _ in range(3)]
        PT = [A.alloc([512], BF16) for _ in range(3)]
        rden = A.alloc([NT], F32)
        m_c1 = A.mark()
        wdkv = A.alloc([8, 256], BF16)
        wdq = A.alloc([8, 384], BF16)
        wkr = A.alloc([8, 32], BF16)
        wkr_rot = A.alloc([8, 32], BF16)
        wuv = A.alloc([2, DM], BF16)
        ckvn = [A.alloc([256], BF16) for _ in range(2)]
        cqn = [A.alloc([384], BF16) for _ in range(2)]
        junk = A.alloc([384], F32)
        ss = [A.alloc([2], F32) for _ in range(2)]
        sd2 = [A.alloc([2], F32) for _ in range(2)]
        rs2 = [A.alloc([2], F32) for _ in range(2)]
        load_w(wdkv, wdkv_d.rearrange("(k p) c -> p k c", p=128), "wdkv")
        load_w(wdq, wdq_d.rearrange("(k p) c -> p k c", p=128), "wdq")
        load_w(wkr, wkr_d.rearrange("(k p) c -> p k c", p=128), "wkr")
        load_w(wuv, wuv_d.rearrange("(k p) c -> p k c", p=128), "wuv")
        load_w(wuk, wuk_d.rearrange("(k p) c -> p k c", p=128), "wuk")
        wuq_src = wuq_d.rearrange("(k p) (h c) -> p k h c", p=128, c=96)
        dma("pool", [(wuq_c[:, k, :, 0:96], wuq_src[:, k, :, :]) for k in range(3)], [], ["wuq"], "wuq")
        dma("sp", [(ropec[64:96, :], ropec_d)], [], ["ropec"], "ropec")
        dma("sp", [(ropes[64:96, :], ropes_d), (ropes[96:128, :], ropes_d)], [], ["ropes"], "ropes")
        memset("pool", V_aug[:, :, :, 64:65], 1.0, [], ["V_ones"])
        if c1_parts < 1:
            P.barrier()
            return
        act(wkr_rot[:, :, 0:16], wkr[:, :, 16:32], AF.Identity, ["wkr"], ["wkr_rot_a"], scale=-1.0)
        act(wkr_rot[:, :, 16:32], wkr[:, :, 0:16], AF.Identity, ["wkr", "wkr_rot_a"], ["wkr_rot"])
        for k in range(3):
            act(wuq_c[:, k, :, 96:112], wuq_c[:, k, :, 80:96], AF.Identity, ["wuq"], [("wuq_c1", k)], scale=-1.0)
            act(wuq_c[:, k, :, 112:128], wuq_c[:, k, :, 64:80], AF.Identity, ["wuq", ("wuq_c1", k)],
                [("wuq_rot", k)])
        rotkeys = [("wuq_rot", k) for k in range(3)]
        def c1_stage_a(tt):
            slot = tt % 2
            bA, bB = (0, 1) if slot == 0 else (2, 3)
            for (bk, w_, ncol, wk_) in ((bA, wdkv, 256, "wdkv"), (bB, wdq, 384, "wdq")):
                for k in range(8):
                    mm(bank(bk, ncol), xTb[:, k, tt * 128:(tt + 1) * 128], w_[:, k, :], k == 0, k == 7,
                       [("xTb", tt), wk_], [("ps", bk)], signal=(k == 7))
            act(junk[:, 0:256], bank(bA, 256), AF.Square, [("ps", bA)], ["junk", ("ss0", slot)],
                accum_out=ss[slot][:, 0:1])
            act(junk[:, 0:384], bank(bB, 384), AF.Square, [("ps", bB), "junk"], ["junk", ("ss1", slot)],
                accum_out=ss[slot][:, 1:2])
            act(sd2[slot][:, 0:1], ss[slot][:, 0:1], AF.Sqrt, [("ss0", slot), "epsc"], [("sd0", slot)],
                bias=epsrms, scale=1.0 / 256.0)
            act(sd2[slot][:, 1:2], ss[slot][:, 1:2], AF.Sqrt, [("ss1", slot), "epsc"], [("sd1", slot)],
                bias=epsrms, scale=1.0 / 384.0)
            recip(rs2[slot], sd2[slot], [("sd0", slot), ("sd1", slot)], [("rs2", slot)])
            stt(ckvn[slot], bank(bA, 256), rs2[slot][:, 0:1], gkv_bc, ALU.mult, ALU.mult,
                [("ps", bA), ("rs2", slot), "gkv"], [("ckvn", slot)])
            stt(cqn[slot], bank(bB, 384), rs2[slot][:, 1:2], gq_bc, ALU.mult, ALU.mult,
                [("ps", bB), ("rs2", slot), "gq"], [("cqn", slot)])

        def c1_stage_b(tt):
            slot = tt % 2
            tb = 4 + slot
            pb_ = bank_bf(tb).rearrange("p (c n) -> p c n", n=128)
            for c in range(2):
                tr(pb_[:, c, :], ckvn[slot][:, c * 128:(c + 1) * 128], [("ckvn", slot)], [("ps", tb)], signal=False)
            for c in range(3):
                tr(pb_[:, 2 + c, :], cqn[slot][:, c * 128:(c + 1) * 128], [("cqn", slot)], [("ps", tb)],
                   signal=(c == 2))
            ceng = "dve" if slot == 0 else "act"
            cp(ceng, ckvT[:, :, tt * 128:(tt + 1) * 128], pb_[:, 0:2, :], [("ps", tb)], [("ckvT", tt)])
            cp(ceng, cqT[:, :, tt * 128:(tt + 1) * 128], pb_[:, 2:5, :], [("ps", tb)], [("cqT", tt)])
            for half in range(2):
                bk = 6 + half
                for c in range(2):
                    mm(bank(bk), ckvT[:, c, tt * 128:(tt + 1) * 128], wuv[:, c, half * 512:(half + 1) * 512],
                       c == 0, c == 1, [("ckvT", tt), "wuv"], [("ps", bk)], signal=(c == 1))
                v_evac(bk, V_aug, tt, half)

        if c1_parts < 2:
            P.barrier()
            return
        c1_stage_a(0)
        for tt in range(NT):
            if tt + 1 < NT:
                c1_stage_a(tt + 1)
            c1_stage_b(tt)
        for tc in range(4 if c1_parts >= 5 else 0):
            sl = tc % 2
            bA, bB = (0, 1) if sl == 0 else (2, 3)
            cs = slice(tc * 512, (tc + 1) * 512)
            for (bk, w_, wk_) in ((bA, wkr, "wkr"), (bB, wkr_rot, "wkr_rot")):
                for k in range(8):
                    mm(ps[64:96, bk * 512:(bk + 1) * 512], w_[:, k, :], xTb[:, k, cs], k == 0, k == 7,
                       [wk_] + xkeys(tc * 512, (tc + 1) * 512), [("ps", bk)], signal=(k == 7))
            tt_("dve", t1[sl][64:96, :], ps[64:96, bA * 512:(bA + 1) * 512], ropec[64:96, cs], ALU.mult,
                [("ps", bA), "ropec"], [("t1", sl)])
            tt_("dve", t2[sl][64:96, :], ps[64:96, bB * 512:(bB + 1) * 512], ropes[64:96, cs], ALU.mult,
                [("ps", bB), "ropes"], [("t2", sl)])
            tt_("pool", KT[0][64:96, cs], t1[sl][64:96, :], t2[sl][64:96, :], ALU.add,
                [("t1", sl), ("t2", sl)], [("KTr", 0, tc)])
            tt_("pool", KT[1][64:96, cs], t1[sl][64:96, :], t2[sl][64:96, :], ALU.add,
                [("t1", sl), ("t2", sl)], [("KTr", 1, tc)])
        P.barrier()
        A.release(m_c1)
        if stop_after == "C1":
            if debug_out:
                dma("sp", [(dbg_d.rearrange("(t p) c -> p t c", p=128)[:, :, 0:64], V_aug[:, :, 0, 0:64]),
                           (dbg_d.rearrange("(t p) c -> p t c", p=128)[0:96, :, 128:256], KT[0][0:96, :].rearrange("p (t c) -> p t c", c=128))], [], [], "dbg")
            return
        wo_off = A.mark()
        wo_c = A.alloc([8, DM], BF16)
        load_w(wo_c, bwo_d.rearrange("(k p) c -> p k c", p=128), "wo", parts=2)
        scale1 = 96.0 ** -0.5
        allckv = [("ckvT", t) for t in range(NT)]
        allcq = [("cqT", t) for t in range(NT)]

        def proj_head_piece(h, tc, part):
            slot = h % 2
            cs = slice(tc * 512, (tc + 1) * 512)
            bA, bB = 6, 7
            sl = tc % 2
            if part == 0:
                c_lo, r_lo = (h * 64, 0) if h + 1 < H else ((h - 1) * 64, 64)
                for c in range(2):
                    mm(bank(bB), wuk[:, c, c_lo:c_lo + 128], ckvT[:, c, cs],
                       c == 0, c == 1, ["wuk"] + allckv[tc * 4:(tc + 1) * 4], [("ps", bB)], signal=(c == 1))
                cp("dve", KT[slot][0:64, cs], ps[r_lo:r_lo + 64, bB * 512:(bB + 1) * 512], [("ps", bB)],
                   [("KT", slot)])
            else:
                for c in range(3):
                    mm(bank(bA), wuq_c[:, c, h, :], cqT[:, c, cs],
                       c == 0, c == 2, rotkeys + allcq[tc * 4:(tc + 1) * 4], [("ps", bA)], signal=(c == 2))
                cp("dve", QT[slot][0:64, cs], ps[0:64, bA * 512:(bA + 1) * 512], [("ps", bA)], [("QT", slot)])
                tt_("dve", t1[sl][64:96, :], ps[64:96, bA * 512:(bA + 1) * 512], ropec[64:96, cs], ALU.mult,
                    [("ps", bA), "ropec"], [("t1", sl)])
                tt_("dve", t2[sl][64:96, :], ps[96:128, bA * 512:(bA + 1) * 512], ropes[96:128, cs], ALU.mult,
                    [("ps", bA), "ropes"], [("t2", sl)])
                tt_("pool", QT[slot][64:96, cs], t1[sl][64:96, :], t2[sl][64:96, :], ALU.add,
                    [("t1", sl), ("t2", sl)], [("QT", slot)])

        for tc in range(4):
            for part in range(2):
                proj_head_piece(0, tc, part)
        for h in range(H):
            slot = h % 2
            hooks = []
            if h + 1 < H:
                hooks = [(lambda h=h, tc=tc, part=part: proj_head_piece(h + 1, tc, part))
                         for tc in range(4) for part in range(2)]
            attention(h, QT[slot][0:96, :], KT[slot][0:96, :], ("QT", slot), ("KT", slot),
                      scale1, False, None, None, V_aug, O_tm, Sb, PT, rden, hooks=hooks)
        P.barrier()
        A.release(m_c2)
        if stop_after == "C2":
            if debug_out:
                dma("sp", [(dbg_d.rearrange("(t p) c -> p t c", p=128), O_tm)], [], [], "dbg")
            return
        mixer_epilogue(s, wo_c, 1, O_tm, xr[1], xr[2])
        assert A.top <= wo_off, "epilogue tiles overlap the prefetched w_o"
        P.barrier()
        A.release(m_c)

    order = ["P0", "A2", "A", "B", "C1", "C2", "C", "D"]
    lim = order.index(stop_after) if stop_after else len(order)
    for s in range(NSEQ if stop_after is None else 1):
        if s > 0:
            P.barrier()
            P.new_engine_sems()
        phase_p0(s)
        if lim == 0:
            break
        if skip_ab:
            phase_c(s)
            break
        phase_a(s)
        if lim <= 2:
            if debug_out and stop_after == "A":
                dma("sp", [(dbg_d.rearrange("(k p) t -> p k t", p=128), xTb)], [], [], "dbg")
            break
        ffn(s, 0, xr[0], xr[1], False)
        if lim == 3:
            if debug_out:
                dma("sp", [(dbg_d.rearrange("(k p) t -> p k t", p=128), xTb)], [], [], "dbg")
            break
        phase_c(s)
        if lim <= 6:
            if debug_out and stop_after == "C":
                pass
            break
        ffn(s, 1, xr[2], y_d, True)

    P.barrier()
    P.emit(block)
    blockc.__exit__(None, None, None)
    psc.__exit__(None, None, None)
    sb.__exit__(None, None, None)
    return nc


def _constants():
    c = {}
    c["c_ident"] = np.eye(128, dtype=np.float32).astype(ml_dtypes.bfloat16)
    ki = np.arange(128)[:, None]
    cc = np.arange(SEQ)[None, :]
    dist = cc - ki
    c["c_dtab"] = dist.astype(np.float32)
    mult = ((dist >= 0) & (dist <= 128)).astype(np.int64)
    mult = mult + ((dist >= 0) & (dist % 4 == 0) & (dist <= 512))
    mult = mult + ((dist >= 0) & (dist % 16 == 0) & (dist <= 2048))
    lt = np.where(mult > 0, 8.0 * np.log(np.maximum(mult, 1).astype(np.float64)), NEG)
    c["c_ltab"] = lt.astype(np.float32)
    j = np.arange(128)[None, :]
    c["c_cmask"] = np.where(j >= ki, 0.0, NEG).astype(np.float32)
    c["c_cmaskb"] = np.where(j >= ki, 0.0, -30000.0).astype(np.float32).astype(ml_dtypes.bfloat16)
    inv_freq = (np.float32(10000.0) ** (-(np.arange(0, 32, 2, dtype=np.float32)) / np.float32(32.0))).astype(np.float32)
    ang = (np.arange(SEQ, dtype=np.float32)[:, None] * inv_freq[None, :]).astype(np.float32)
    cos = np.cos(ang).astype(np.float32).T
    sin = np.sin(ang).astype(np.float32).T
    c["c_ropec"] = np.ascontiguousarray(np.concatenate([cos, cos], axis=0))
    c["c_ropes"] = np.ascontiguousarray(np.concatenate([sin, sin], axis=0))
    return c


def make_in_maps(inputs):
    f = lambda a: np.ascontiguousarray(np.asarray(a, dtype=np.float32))
    x = f(inputs["x"]).reshape(NCORES, NSEQ * SEQ, DM)
    shared = {
        "a_w_qkv": f(inputs["a_w_qkv"])[0],
        "a_w_o": f(inputs["a_w_o"])[0],
        "kv_w_dkv": f(inputs["kv_w_dkv"]),
        "kv_norm_g": f(inputs["kv_norm_g"]).reshape(1, 256),
        "kv_w_kr": f(inputs["kv_w_kr"]),
        "kv_w_uk": f(inputs["kv_w_uk"]),
        "kv_w_uv": f(inputs["kv_w_uv"]),
        "b_w_dq": f(inputs["b_w_dq"])[0],
        "b_q_norm_g": f(inputs["b_q_norm_g"]).reshape(1, 384),
        "b_w_uq": f(inputs["b_w_uq"])[0],
        "b_w_o": f(inputs["b_w_o"])[0],
        "ffn_w_in": f(inputs["ffn_w_in"]),
        "ffn_w_out": f(inputs["ffn_w_out"]),
        "ln_mix_g": f(inputs["ln_mix_g"]),
        "ln_mix_b": f(inputs["ln_mix_b"]),
        "ln_ffn_g": f(inputs["ln_ffn_g"]),
        "ln_ffn_b": f(inputs["ln_ffn_b"]),
    }
    cwv = f(inputs["ffn_conv_w"]).reshape(2, 3, 2 * NG, 128)
    shared["ffn_conv_w"] = np.ascontiguousarray(cwv.transpose(3, 0, 2, 1)).reshape(128, -1)
    cbv = f(inputs["ffn_conv_b"]).reshape(2, 2 * NG, 128)
    shared["ffn_conv_b"] = np.ascontiguousarray(cbv.transpose(2, 0, 1)).reshape(128, -1)
    shared.update(_constants())
    return [dict(shared, x=np.ascontiguousarray(x[i])) for i in range(NCORES)]


_NC_CACHE = {}


def kernel(**inputs):
    in_maps = make_in_maps(inputs)
    if "nc" not in _NC_CACHE:
        _NC_CACHE["nc"] = build_program()
    nc = _NC_CACHE["nc"]
    res = run_bass_kernel_spmd(nc, in_maps, core_ids=list(range(NCORES)))
    out = np.stack([np.asarray(r["y"], dtype=np.float32) for r in res.results], axis=0)
    return out.reshape(NCORES * NSEQ, SEQ, DM)
```
